# Optimizing a Trainium2 kernel written in Bass

```python
import math, functools
import jax, jax.numpy as jnp
from jax import lax
import numpy as np

D_MODEL = 4096
BATCH = 4
SEQ = 4096
DEPTH = 1

N_META = 16
Q_BLOCK = 128
FOX_HEADS = 16
FOX_HEAD_DIM = 128
DIFF_HEADS = 8
DIFF_HEAD_DIM = 128
D_FF = 11008
RMS_EPS = 1e-6
ALIBI_MAX_BIAS = 8.0

FOX_WIDTH = FOX_HEADS * FOX_HEAD_DIM
DIFF_QK_WIDTH = DIFF_HEADS * 2 * DIFF_HEAD_DIM
DIFF_V_WIDTH = DIFF_HEADS * 2 * DIFF_HEAD_DIM
IN_SPLITS = [FOX_WIDTH, 2 * FOX_WIDTH, 3 * FOX_WIDTH, 3 * FOX_WIDTH + FOX_HEADS,
             3 * FOX_WIDTH + FOX_HEADS + DIFF_QK_WIDTH,
             3 * FOX_WIDTH + FOX_HEADS + 2 * DIFF_QK_WIDTH,
             3 * FOX_WIDTH + FOX_HEADS + 2 * DIFF_QK_WIDTH + DIFF_V_WIDTH,
             3 * FOX_WIDTH + FOX_HEADS + 2 * DIFF_QK_WIDTH + DIFF_V_WIDTH + D_MODEL]
IN_COLS = IN_SPLITS[-1] + D_MODEL

kernel_name = "fox_diffattn_gated_macaron_block"


def rms_norm(x, g):
    xf = x.astype(jnp.float32)
    y = xf * lax.rsqrt(jnp.mean(xf * xf, axis=-1, keepdims=True) + RMS_EPS)
    return (y * g.astype(jnp.float32)).astype(x.dtype)


def swiglu(x, w_gate, w_up, w_down):
    return (jax.nn.silu(x @ w_gate) * (x @ w_up)) @ w_down


def sweep_query_blocks(block_fn, q_args):
    L = q_args[0].shape[1]
    n_blocks = (L - N_META) // Q_BLOCK
    meta_out = block_fn(tuple(a[:, :N_META] for a in q_args), jnp.arange(N_META, dtype=jnp.int32))

    def to_blocks(a):
        a = a[:, N_META:]
        a = a.reshape(a.shape[0], n_blocks, Q_BLOCK, *a.shape[2:])
        return jnp.moveaxis(a, 1, 0)

    t_real = N_META + jnp.arange(n_blocks * Q_BLOCK, dtype=jnp.int32).reshape(n_blocks, Q_BLOCK)
    real_out = lax.map(lambda xs: block_fn(xs[0], xs[1]),
                       (tuple(to_blocks(a) for a in q_args), t_real))
    real_out = jnp.moveaxis(real_out, 0, 1)
    real_out = real_out.reshape(real_out.shape[0], n_blocks * Q_BLOCK, *real_out.shape[3:])
    return jnp.concatenate([meta_out, real_out], axis=1)


def fox_block(q_args, t_q, k, v, c_k):
    q, c_q = q_args
    k_pos = jnp.arange(k.shape[1], dtype=jnp.int32)
    s = jnp.einsum('bqhd,bkhd->bhqk', q, k, preferred_element_type=jnp.float32) * (FOX_HEAD_DIM ** -0.5)
    s = s + jnp.swapaxes(c_q, 1, 2)[..., None] - jnp.swapaxes(c_k, 1, 2)[:, :, None, :]
    causal = t_q[:, None] >= k_pos[None, :]
    s = jnp.where(causal, s, -jnp.inf)
    p = jax.nn.softmax(s, axis=-1).astype(v.dtype)
    return jnp.einsum('bhqk,bkhd->bqhd', p, v)


def diff_block(q_args, t_q, k, v, lam, slopes):
    (q,) = q_args
    k_pos = jnp.arange(k.shape[1], dtype=jnp.int32)
    s = jnp.einsum('bqhcd,bkhcd->bhcqk', q, k, preferred_element_type=jnp.float32) * (DIFF_HEAD_DIM ** -0.5)
    dist = (t_q[:, None] - k_pos[None, :]).astype(jnp.float32)
    s = s - slopes[None, :, None, None, None] * dist[None, None, None]
    causal = t_q[:, None] >= k_pos[None, :]
    s = jnp.where(causal, s, -jnp.inf)
    p = jax.nn.softmax(s, axis=-1)
    a = p[:, :, 0] - lam * p[:, :, 1]
    return jnp.einsum('bhqk,bkhd->bqhd', a.astype(v.dtype), v)


def hybrid_mixer(u, w_in, b_forget, lambda_q1, lambda_k1, lambda_q2, lambda_k2, diff_subln_g,
                 w_o_fox, w_o_diff, w_out, lambda_init):
    B, L, _ = u.shape
    proj = u @ w_in
    fq, fk, fv, f_logit, dq, dk, dv, g_fox, g_diff = jnp.split(proj, IN_SPLITS, axis=-1)

    fq = fq.reshape(B, L, FOX_HEADS, FOX_HEAD_DIM)
    fk = fk.reshape(B, L, FOX_HEADS, FOX_HEAD_DIM)
    fv = fv.reshape(B, L, FOX_HEADS, FOX_HEAD_DIM)
    log_f = jax.nn.log_sigmoid(f_logit.astype(jnp.float32) + b_forget.astype(jnp.float32))
    c = jnp.cumsum(log_f, axis=1)
    fox_out = sweep_query_blocks(functools.partial(fox_block, k=fk, v=fv, c_k=c), (fq, c))

    dq = dq.reshape(B, L, DIFF_HEADS, 2, DIFF_HEAD_DIM)
    dk = dk.reshape(B, L, DIFF_HEADS, 2, DIFF_HEAD_DIM)
    dv = dv.reshape(B, L, DIFF_HEADS, 2 * DIFF_HEAD_DIM)
    f32 = jnp.float32
    lam = (jnp.exp(jnp.sum(lambda_q1.astype(f32) * lambda_k1.astype(f32)))
           - jnp.exp(jnp.sum(lambda_q2.astype(f32) * lambda_k2.astype(f32))) + lambda_init)
    slopes = jnp.exp2(-ALIBI_MAX_BIAS * jnp.arange(1, DIFF_HEADS + 1, dtype=f32) / DIFF_HEADS)
    diff_out = sweep_query_blocks(functools.partial(diff_block, k=dk, v=dv, lam=lam, slopes=slopes), (dq,))
    diff_out = rms_norm(diff_out, diff_subln_g) * (1.0 - lambda_init)

    y_fox = fox_out.reshape(B, L, FOX_WIDTH) @ w_o_fox
    y_diff = diff_out.reshape(B, L, DIFF_V_WIDTH) @ w_o_diff
    merged = jax.nn.sigmoid(g_fox) * y_fox + jax.nn.sigmoid(g_diff) * y_diff
    return merged @ w_out


def setup_inputs(seed: int = 0) -> dict:
    key = jax.random.key(seed)
    ks = jax.random.split(key, 24)
    f32 = jnp.float32

    def dense(k, shape):
        return jax.random.normal(k, shape, f32) * (shape[-2] ** -0.5)

    def gain(k, n):
        return 1.0 + 0.01 * jax.random.normal(k, (DEPTH, n), f32)

    return {
        "x": jax.random.normal(ks[0], (BATCH, SEQ, D_MODEL), f32),
        "meta_tokens": jax.random.normal(ks[1], (N_META, D_MODEL), f32),
        "ff1_pre_g": gain(ks[2], D_MODEL),
        "ff1_w_gate": dense(ks[3], (DEPTH, D_MODEL, D_FF)),
        "ff1_w_up": dense(ks[4], (DEPTH, D_MODEL, D_FF)),
        "ff1_w_down": dense(ks[5], (DEPTH, D_FF, D_MODEL)),
        "ff1_post_g": gain(ks[6], D_MODEL),
        "mix_pre_g": gain(ks[7], D_MODEL),
        "w_in": dense(ks[8], (DEPTH, D_MODEL, IN_COLS)),
        "b_forget": jax.random.uniform(ks[9], (DEPTH, FOX_HEADS), f32, 1.0, 3.0),
        "lambda_q1": 0.1 * jax.random.normal(ks[10], (DEPTH, DIFF_HEAD_DIM), f32),
        "lambda_k1": 0.1 * jax.random.normal(ks[11], (DEPTH, DIFF_HEAD_DIM), f32),
        "lambda_q2": 0.1 * jax.random.normal(ks[12], (DEPTH, DIFF_HEAD_DIM), f32),
        "lambda_k2": 0.1 * jax.random.normal(ks[13], (DEPTH, DIFF_HEAD_DIM), f32),
        "diff_subln_g": gain(ks[14], 2 * DIFF_HEAD_DIM),
        "w_o_fox": dense(ks[15], (DEPTH, FOX_WIDTH, D_MODEL)),
        "w_o_diff": dense(ks[16], (DEPTH, DIFF_V_WIDTH, D_MODEL)),
        "w_out": dense(ks[17], (DEPTH, D_MODEL, D_MODEL)),
        "mix_post_g": gain(ks[18], D_MODEL),
        "ff2_pre_g": gain(ks[19], D_MODEL),
        "ff2_w_gate": dense(ks[20], (DEPTH, D_MODEL, D_FF)),
        "ff2_w_up": dense(ks[21], (DEPTH, D_MODEL, D_FF)),
        "ff2_w_down": dense(ks[22], (DEPTH, D_FF, D_MODEL)),
        "ff2_post_g": gain(ks[23], D_MODEL),
    }


def reference(x, meta_tokens, ff1_pre_g, ff1_w_gate, ff1_w_up, ff1_w_down, ff1_post_g,
              mix_pre_g, w_in, b_forget, lambda_q1, lambda_k1, lambda_q2, lambda_k2, diff_subln_g,
              w_o_fox, w_o_diff, w_out, mix_post_g,
              ff2_pre_g, ff2_w_gate, ff2_w_up, ff2_w_down, ff2_post_g):
    B = x.shape[0]
    meta = jnp.broadcast_to(meta_tokens.astype(x.dtype)[None], (B, N_META, x.shape[-1]))
    h = jnp.concatenate([meta, x], axis=1)
    for layer in range(DEPTH):
        lambda_init = 0.8 - 0.6 * math.exp(-0.3 * layer)
        h = h + 0.5 * rms_norm(swiglu(rms_norm(h, ff1_pre_g[layer]), ff1_w_gate[layer],
                                      ff1_w_up[layer], ff1_w_down[layer]), ff1_post_g[layer])
        mix = hybrid_mixer(rms_norm(h, mix_pre_g[layer]), w_in[layer], b_forget[layer],
                           lambda_q1[layer], lambda_k1[layer], lambda_q2[layer], lambda_k2[layer],
                           diff_subln_g[layer], w_o_fox[layer], w_o_diff[layer], w_out[layer],
                           lambda_init)
        h = h + rms_norm(mix, mix_post_g[layer])
        h = h + 0.5 * rms_norm(swiglu(rms_norm(h, ff2_pre_g[layer]), ff2_w_gate[layer],
                                      ff2_w_up[layer], ff2_w_down[layer]), ff2_post_g[layer])
    return h[:, N_META:]
```

```python
import contextlib
import math
import numpy as np
import concourse.bass as bass
import concourse.mybir as mybir
from concourse.bass_utils import run_bass_kernel_spmd

F32 = mybir.dt.float32
BF16 = mybir.dt.bfloat16
AF = mybir.ActivationFunctionType
ALU = mybir.AluOpType

FULL = dict(D=4096, DFF=11008, TQ=2048, NM=16, FH=16, DH=8, B=4)
RMS_EPS = 1e-6
LAMBDA_INIT = 0.8 - 0.6 * math.exp(-0.3 * 0)
NEG = -30000.0


class Buf:
    __slots__ = ("name", "lw", "rd")

    def __init__(self, name):
        self.name = name
        self.lw = None
        self.rd = []


class Chan:
    def __init__(self, nc, name, step):
        self.sem = nc.alloc_semaphore(name)
        self.name = name
        self.step = step
        self.val = 0


class Eng:
    def __init__(self, nc, name, handle, self_sync):
        self.name = name
        self.h = handle
        self.chan = Chan(nc, "s_" + name, 1)
        self.waited = {}
        self.self_sync = self_sync


class Sched:
    def __init__(self, nc):
        self.nc = nc
        self.pe = Eng(nc, "pe", nc.tensor, False)
        self.act = Eng(nc, "act", nc.scalar, True)
        self.dve = Eng(nc, "dve", nc.vector, True)
        self.pool = Eng(nc, "pool", nc.gpsimd, True)
        self.sp = Eng(nc, "sp", nc.sync, True)
        self.engs = [self.pe, self.act, self.dve, self.pool, self.sp]
        self.chans = [e.chan for e in self.engs]
        self.free = []
        self.inphase = []

    def dma_chan(self, name, persistent=True):
        if not persistent and self.free:
            c = self.free.pop()
        else:
            c = Chan(self.nc, name, 16)
            self.chans.append(c)
        if not persistent:
            self.inphase.append(c)
        return c

    def release_phase(self):
        self.free.extend(self.inphase)
        self.inphase = []

    def _deps(self, reads, writes):
        deps = {}

        def add(d):
            if d is None:
                return
            c, v = d
            if deps.get(c, 0) < v:
                deps[c] = v
        for b in reads:
            add(b.lw)
        for b in writes:
            add(b.lw)
            for r in b.rd:
                add(r)
        return deps

    def _wait(self, eng, deps):
        for c, v in deps.items():
            if c is eng.chan and not eng.self_sync:
                continue
            if eng.waited.get(c, 0) >= v:
                continue
            eng.h.wait_ge(c.sem, v)
            eng.waited[c] = v

    def _mark(self, tok, reads, writes):
        for b in reads:
            b.rd.append(tok)
            if len(b.rd) > 24:
                best = {}
                for c, v in b.rd:
                    if best.get(c, 0) < v:
                        best[c] = v
                b.rd = list(best.items())
        for b in writes:
            b.lw = tok
            b.rd = []

    def op(self, eng, fn, reads=(), writes=()):
        self._wait(eng, self._deps(reads, writes))
        inst = fn()
        eng.chan.val += 1
        inst.then_inc(eng.chan.sem, 1)
        self._mark((eng.chan, eng.chan.val), reads, writes)

    def group(self, eng, fns, reads=(), writes=()):
        self._wait(eng, self._deps(reads, writes))
        inst = None
        for fn in fns:
            inst = fn()
        eng.chan.val += 1
        inst.then_inc(eng.chan.sem, 1)
        self._mark((eng.chan, eng.chan.val), reads, writes)

    def dma(self, eng, chan, fn, reads=(), writes=()):
        self._wait(eng, self._deps(reads, writes))
        inst = fn()
        chan.val += 16
        inst.then_inc(chan.sem, 16)
        self._mark((chan, chan.val), reads, writes)

    def barrier(self):
        for e in self.engs:
            for c in self.chans:
                if c.val == 0 or e.waited.get(c, 0) >= c.val:
                    continue
                e.h.wait_ge(c.sem, c.val)
                e.waited[c] = c.val
        self.release_phase()


class Slot:
    _uid = [0]

    def __init__(self, S, nc, name, shape, dtype, stack=None):
        Slot._uid[0] += 1
        name = f"{name}_{Slot._uid[0]}"
        if stack is not None:
            t = stack.enter_context(nc.sbuf_tensor(name, list(shape), dtype))
        else:
            t = nc.alloc_sbuf_tensor(name, list(shape), dtype)
        self.ap = t.ap() if callable(getattr(t, "ap", None)) else t
        self.buf = Buf(name)
        self.chan = None
        self._S = S
        self._name = name
        self._persistent = stack is None

    def ch(self):
        if self.chan is None:
            self.chan = self._S.dma_chan("c_" + self._name, self._persistent)
        return self.chan


class Ring:
    def __init__(self, slots):
        self.slots = slots
        self.i = 0

    def next(self):
        s = self.slots[self.i % len(self.slots)]
        self.i += 1
        return s


class Builder:
    def __init__(self, cfg):
        self.cfg = cfg
        D, DFF, TQ, NM, FH, DH = (cfg[k] for k in ("D", "DFF", "TQ", "NM", "FH", "DH"))
        self.D, self.DFF, self.TQ, self.NM, self.FH, self.DH = D, DFF, TQ, NM, FH, DH
        self.DC = D // 128
        self.FC = DFF // 128
        self.FW = FH * 128
        self.DQ = DH * 256
        self.AC = (self.FW + self.DQ) // 128
        self.TO = TQ + NM
        self.LK = self.TO + TQ
        self.NQB = TQ // 128
        self.NKB = 2 * self.NQB + 1
        self.MKB = self.NQB
        FW, DQ = self.FW, self.DQ
        self.c_fq, self.c_fk, self.c_fv = 0, FW, 2 * FW
        self.c_fl = 3 * FW
        self.c_dq = 3 * FW + FH
        self.c_dk = self.c_dq + DQ
        self.c_dv = self.c_dk + DQ
        self.c_gf = self.c_dv + DQ
        self.c_gd = self.c_gf + D
        self.INC = self.c_gd + D

        nc = bass.Bass("TRN2", target_bir_lowering=False)
        self.nc = nc
        self.S = Sched(nc)

        def din(name, shape, dt=F32):
            return nc.dram_tensor(name, list(shape), dt, kind="ExternalInput").ap()

        def dscr(name, shape, dt):
            return nc.dram_tensor(name, list(shape), dt, kind="Internal").ap()

        TO, LK, DC = self.TO, self.LK, self.DC
        self.xT_own = din("xT_own", [D, TQ])
        self.xT_oth = din("xT_oth", [D, TO])
        self.w = {}
        for nm, shp in (("ff1_w_gate", [D, DFF]), ("ff1_w_up", [D, DFF]), ("ff1_w_down", [DFF, D]),
                        ("w_in", [D, self.INC]), ("w_o_fox", [FW, D]), ("w_o_diff", [DQ, D]),
                        ("w_out", [D, D]), ("ff2_w_gate", [D, DFF]), ("ff2_w_up", [D, DFF]),
                        ("ff2_w_down", [DFF, D])):
            self.w[nm] = din(nm, shp)
        self.gains_d = din("gains", [128, 6 * DC])
        self.bfg_d = din("bfg", [FH, 1])
        self.lamv_d = din("lamv", [128, 4 * 128])
        self.subg_d = din("subg", [128, 2])
        self.tri_d = din("tri", [128, 128])
        self.sel_d = din("sel", [FH, FH * 128])
        self.ident_d = din("ident", [FH, FH])
        self.kpos_d = din("kpos", [128, self.NKB])
        self.exclk_d = din("exclk", [128, self.NKB])
        self.qpos0_d = din("qpos0", [128, self.NQB])
        self.apf_d = din("apf", [FH, 1])
        self.slopes_d = din("slopes", [128, DH])
        self.outT = nc.dram_tensor("outT", [D, TQ], F32, kind="ExternalOutput").ap()
        self.hidT = dscr("hidT", [DFF, TO], BF16)
        self.uTd = dscr("uTd", [D, TO], BF16)
        self.ssqd = dscr("ssqd", [128, TO], F32)
        self.yT = dscr("yT", [D, TO], F32)
        self.h1T = dscr("h1T", [D, TQ], F32)
        self.h2T = dscr("h2T", [D, TQ], F32)
        self.KfT = dscr("KfT", [FW, LK], BF16)
        self.KdT = dscr("KdT", [DQ, LK], BF16)
        self.QfT = dscr("QfT", [FW, TQ], BF16)
        self.QdT = dscr("QdT", [DQ, TQ], BF16)
        self.Vtok = dscr("Vtok", [self.NKB * 128, FW + DQ], BF16)
        self.zT = dscr("zT", [FH, LK], F32)
        self.SG = dscr("SG", [2 * D, TQ], BF16)
        self.attT = dscr("attT", [FW + DQ, TQ], BF16)
        self.mrgT = dscr("mrgT", [D, TQ], BF16)
        self.mixT = dscr("mixT", [D, TQ], F32)
        self.dbufs = {}

        S = self.S
        self.NKW = 43
        self.wring = Ring([Slot(S, nc, f"wr{i}", [128, self.NKW, 128], BF16) for i in range(4)])
        self.psum = []
        for i in range(8):
            t = nc.psum_tensor(f"ps{i}", [128, 512], F32).__enter__()
            self.psum.append((t.ap() if callable(getattr(t, "ap", None)) else t, Buf(f"ps{i}")))
        self.ps_i = 0
        self.cst = {}
        self.cbuf = Buf("consts")
        self.cchan = S.dma_chan("c_consts")

    def dbuf(self, key):
        b = self.dbufs.get(key)
        if b is None:
            b = self.dbufs[key] = Buf(str(key))
        return b

    def bank(self):
        p = self.psum[self.ps_i % 8]
        self.ps_i += 1
        return p

    def tiles(self, T):
        out, t = [], 0
        while t < T:
            n = min(512, T - t)
            out.append((t, n))
            t += n
        return out

    def cload(self, name, src, shape, dt=F32, cast=False):
        nc, S = self.nc, self.S
        t = nc.sbuf_tensor("k_" + name, list(shape), dt).__enter__()
        ap = t.ap() if callable(getattr(t, "ap", None)) else t
        if cast:
            S.dma(S.pool, self.cchan, lambda: nc.gpsimd.dma_start(out=ap, in_=src), writes=[self.cbuf])
        else:
            S.dma(S.sp, self.cchan, lambda: nc.sync.dma_start(out=ap, in_=src), writes=[self.cbuf])
        self.cst[name] = ap
        return ap

    def load_consts(self):
        nc, S = self.nc, self.S
        DC, FH = self.DC, self.FH
        g = self.cload("gains", self.gains_d, [128, 6 * DC])
        self.gain = lambda i: g[:, i * DC:(i + 1) * DC]
        self.cload("bfg", self.bfg_d, [FH, 1])
        self.cload("lamv", self.lamv_d, [128, 4 * 128])
        self.cload("subg", self.subg_d, [128, 2])
        self.cload("tri", self.tri_d, [128, 128], BF16, cast=True)
        self.cload("ident", self.ident_d, [FH, FH])
        self.cload("kpos", self.kpos_d, [128, self.NKB])
        self.cload("exclk", self.exclk_d, [128, self.NKB])
        self.cload("qpos0", self.qpos0_d, [128, self.NQB])
        self.cload("apf", self.apf_d, [FH, 1])
        self.cload("slopes", self.slopes_d, [128, self.DH])
        t = nc.sbuf_tensor("k_ones", [128, 128], BF16).__enter__()
        ones = t.ap() if callable(getattr(t, "ap", None)) else t
        S.op(S.dve, lambda: nc.vector.memset(ones, 1.0), writes=[self.cbuf])
        self.cst["ones"] = ones
        t = nc.sbuf_tensor("k_ones32", [128, 128], F32).__enter__()
        ones32 = t.ap() if callable(getattr(t, "ap", None)) else t
        S.op(S.dve, lambda: nc.vector.memset(ones32, 1.0), writes=[self.cbuf])
        self.cst["ones32"] = ones32
        t = nc.sbuf_tensor("k_lam", [128, 8], F32).__enter__()
        lam = t.ap() if callable(getattr(t, "ap", None)) else t
        t = nc.sbuf_tensor("k_lamtmp", [128, 128], F32).__enter__()
        ltmp = t.ap() if callable(getattr(t, "ap", None)) else t
        lv = self.cst["lamv"]
        cb = self.cbuf
        S.op(S.dve, lambda: nc.vector.tensor_tensor(out=ltmp, in0=lv[:, 0:128], in1=lv[:, 128:256], op=ALU.mult), reads=[cb], writes=[cb])
        S.op(S.dve, lambda: nc.vector.reduce_sum(out=lam[:, 0:1], in_=ltmp, axis=mybir.AxisListType.X), reads=[cb], writes=[cb])
        S.op(S.dve, lambda: nc.vector.tensor_tensor(out=ltmp, in0=lv[:, 256:384], in1=lv[:, 384:512], op=ALU.mult), reads=[cb], writes=[cb])
        S.op(S.dve, lambda: nc.vector.reduce_sum(out=lam[:, 1:2], in_=ltmp, axis=mybir.AxisListType.X), reads=[cb], writes=[cb])
        S.op(S.act, lambda: nc.scalar.activation(out=lam[:, 2:4], in_=lam[:, 0:2], func=AF.Exp), reads=[cb], writes=[cb])
        S.op(S.dve, lambda: nc.vector.tensor_tensor(out=lam[:, 4:5], in0=lam[:, 3:4], in1=lam[:, 2:3], op=ALU.subtract), reads=[cb], writes=[cb])
        S.op(S.dve, lambda: nc.vector.tensor_scalar(out=lam[:, 5:6], in0=lam[:, 4:5], scalar1=-LAMBDA_INIT, scalar2=None, op0=ALU.add), reads=[cb], writes=[cb])
        self.neg_lam = lam[:, 5:6]
        t = nc.sbuf_tensor("k_subg2", [128, 2], F32).__enter__()
        sg2 = t.ap() if callable(getattr(t, "ap", None)) else t
        S.op(S.dve, lambda: nc.vector.tensor_scalar(out=sg2, in0=self.cst["subg"], scalar1=1.0 - LAMBDA_INIT, scalar2=None, op0=ALU.mult), reads=[cb], writes=[cb])
        self.cst["subg2"] = sg2

    def wtile(self, w_ap, k0, nk, c0, ncol):
        nc, S = self.nc, self.S
        assert nk <= self.NKW and ncol <= 128
        s = self.wring.next()
        dst = s.ap[:, 0:nk, 0:ncol]
        src = w_ap[k0 * 128:(k0 + nk) * 128, c0:c0 + ncol].rearrange("(c p) f -> p c f", p=128)
        S.dma(S.pool, s.ch(), lambda: nc.gpsimd.dma_start(out=dst, in_=src), writes=[s.buf])
        return s

    def mm_group(self, ps, psbuf, M, n, lhs_list, rhs_list, reads):
        nc, S = self.nc, self.S
        L = len(lhs_list)
        fns = []
        for i in range(L):
            fns.append(lambda i=i: nc.tensor.matmul(ps[:M, :n], lhsT=lhs_list[i], rhs=rhs_list[i],
                                                    start=(i == 0), stop=(i == L - 1)))
        S.group(S.pe, fns, reads=reads, writes=[psbuf])

    def norm_stage(self, st, T, res_src, y_src, coef, g_post, res_dst, g_pre):
        nc, S = self.nc, self.S
        DC, D = self.DC, self.D
        PC = 4 if DC % 4 == 0 else 1
        NP = DC // PC
        ones = self.cst["ones"]
        cb = self.cbuf
        H = Slot(S, nc, "nH", [128, DC, 512], F32, st) if g_pre is not None else None
        rp = Ring([Slot(S, nc, f"nr{i}", [128, PC, 512], F32, st) for i in range(3)])
        yp = Ring([Slot(S, nc, f"ny{i}", [128, PC, 512], F32, st) for i in range(3)])
        sqr = Ring([Slot(S, nc, f"nsq{i}", [128, PC, 512], BF16, st) for i in range(2)])
        upr = Ring([Slot(S, nc, f"nu{i}", [128, PC, 512], BF16, st) for i in range(2)])
        rsd1 = Slot(S, nc, "nrs1", [128, 512], F32, st)
        ssl = Ring([Slot(S, nc, f"nss{i}", [128, 512], F32, st) for i in range(2)])
        gcs = Slot(S, nc, "ngc", [128, DC], F32, st)
        if y_src is not None:
            S.op(S.dve, lambda: nc.vector.tensor_scalar(out=gcs.ap, in0=g_post, scalar1=float(coef), scalar2=None, op0=ALU.mult), reads=[cb], writes=[gcs.buf])
        rsd2 = Slot(S, nc, "nrs2", [128, 512], F32, st)

        def piece(ap, pc, t0, n):
            return ap[pc * PC * 128:(pc + 1) * PC * 128, t0:t0 + n].rearrange("(c p) t -> p c t", p=128)

        def accum_sq(src_view, src_buf, ps, pb, pc, n):
            sq = sqr.next()
            S.op(S.act, lambda: nc.scalar.activation(out=sq.ap[:, :, 0:n], in_=src_view, func=AF.Square), reads=[src_buf], writes=[sq.buf])
            fns = [lambda c=c: nc.tensor.matmul(ps[:, 0:n], lhsT=ones, rhs=sq.ap[:, c, 0:n], start=(pc == 0 and c == 0), stop=(pc == NP - 1 and c == PC - 1))
                   for c in range(PC)]
            S.group(S.pe, fns, reads=[sq.buf, cb], writes=[pb])

        def finish_rstd(ps, pb, rsd, n):
            S.op(S.act, lambda: nc.scalar.activation(out=rsd.ap[:, 0:n], in_=ps[:, 0:n], func=AF.Sqrt, scale=1.0 / D, bias=self.eps_ap), reads=[pb, cb], writes=[rsd.buf])
            S.op(S.dve, lambda: nc.vector.reciprocal(out=rsd.ap[:, 0:n], in_=rsd.ap[:, 0:n]), reads=[rsd.buf], writes=[rsd.buf])
            return rsd.ap[:, 0:n].unsqueeze(1).to_broadcast([128, PC, n])

        for (t0, n) in self.tiles(T):
            if y_src is not None:
                ps1, pb1 = self.bank()
                sl = ssl.next()
                S.dma(S.sp, sl.ch(), lambda: nc.sync.dma_start(out=sl.ap[:, 0:n], in_=self.ssqd[:, t0:t0 + n]), writes=[sl.buf])
                S.group(S.pe, [lambda: nc.tensor.matmul(ps1[:, 0:n], lhsT=self.cst["ones32"], rhs=sl.ap[:, 0:n], start=True, stop=True)],
                        reads=[sl.buf, cb], writes=[pb1])
                rb1 = finish_rstd(ps1, pb1, rsd1, n)
            if g_pre is not None:
                ps2, pb2 = self.bank()
            for pc in range(NP):
                gsl = slice(pc * PC, (pc + 1) * PC)
                if y_src is not None:
                    y = yp.next()
                    yv = y.ap[:, :, 0:n]
                    S.dma(S.sp, y.ch(), lambda: nc.sync.dma_start(out=yv, in_=piece(y_src, pc, t0, n)), writes=[y.buf])
                    r = rp.next()
                    rv = r.ap[:, :, 0:n]
                    S.dma(S.sp, r.ch(), lambda: nc.sync.dma_start(out=rv, in_=piece(res_src, pc, t0, n)), writes=[r.buf])
                    fns = [lambda c=c: nc.scalar.activation(out=y.ap[:, c, 0:n], in_=y.ap[:, c, 0:n], func=AF.Copy, scale=gcs.ap[:, pc * PC + c:pc * PC + c + 1])
                           for c in range(PC)]
                    S.group(S.act, fns, reads=[y.buf, gcs.buf], writes=[y.buf])
                    S.op(S.dve, lambda: nc.vector.tensor_tensor(out=yv, in0=yv, in1=rb1, op=ALU.mult), reads=[y.buf, rsd1.buf], writes=[y.buf])
                    if H is not None:
                        hv, hbuf = H.ap[:, gsl, 0:n], H.buf
                        S.op(S.dve, lambda: nc.vector.tensor_tensor(out=hv, in0=rv, in1=yv, op=ALU.add), reads=[r.buf, y.buf], writes=[H.buf])
                        dma_src_slot = None
                    else:
                        hv, hbuf = rv, r.buf
                        S.op(S.dve, lambda: nc.vector.tensor_tensor(out=rv, in0=rv, in1=yv, op=ALU.add), reads=[r.buf, y.buf], writes=[r.buf])
                    if res_dst is not None:
                        st_slot = r
                        S.dma(S.pool, st_slot.ch(), lambda: nc.gpsimd.dma_start(out=piece(res_dst, pc, t0, n), in_=hv), reads=[hbuf, r.buf], writes=[r.buf] if H is not None else [])
                else:
                    hv, hbuf = H.ap[:, gsl, 0:n], H.buf
                    S.dma(S.sp, H.ch(), lambda: nc.sync.dma_start(out=hv, in_=piece(res_src, pc, t0, n)), writes=[H.buf])
                if g_pre is not None:
                    accum_sq(hv, hbuf, ps2, pb2, pc, n)
            if g_pre is not None:
                rb2 = finish_rstd(ps2, pb2, rsd2, n)
                for pc in range(NP):
                    gsl = slice(pc * PC, (pc + 1) * PC)
                    tmp = yp.next()
                    tv = tmp.ap[:, :, 0:n]
                    S.op(S.dve, lambda: nc.vector.tensor_tensor(out=tv, in0=H.ap[:, gsl, 0:n], in1=rb2, op=ALU.mult), reads=[H.buf, rsd2.buf], writes=[tmp.buf])
                    u = upr.next()
                    uv = u.ap[:, :, 0:n]
                    fns = [lambda c=c: nc.scalar.activation(out=u.ap[:, c, 0:n], in_=tmp.ap[:, c, 0:n], func=AF.Copy, scale=g_pre[:, pc * PC + c:pc * PC + c + 1])
                           for c in range(PC)]
                    S.group(S.act, fns, reads=[tmp.buf, cb], writes=[u.buf])
                    S.dma(S.pool, u.ch(), lambda: nc.gpsimd.dma_start(out=piece(self.uTd, pc, t0, n), in_=uv), reads=[u.buf])

    def load_tiled(self, st, name, src, nch, T):
        nc, S = self.nc, self.S
        u = Slot(S, nc, name, [128, nch, T], BF16, st)
        bufs = []
        for (t0, n) in self.tiles(T):
            bf = Buf(f"{name}{t0}")
            ch = S.dma_chan(f"c_{name}{t0}", persistent=False)
            S.dma(S.sp, ch, lambda: nc.sync.dma_start(out=u.ap[:, :, t0:t0 + n], in_=src[0:nch * 128, t0:t0 + n].rearrange("(c p) t -> p c t", p=128)), writes=[bf])
            bufs.append(bf)
        return u.ap, bufs

    def load_u(self, st, T):
        nc, S = self.nc, self.S
        DC = self.DC
        u = Slot(S, nc, "uT", [128, DC, T], BF16, st)
        bufs = []
        for (t0, n) in self.tiles(T):
            bf = Buf(f"uT{t0}")
            ch = S.dma_chan(f"c_uT{t0}", persistent=False)
            S.dma(S.sp, ch, lambda: nc.sync.dma_start(out=u.ap[:, :, t0:t0 + n], in_=self.uTd[:, t0:t0 + n].rearrange("(c p) t -> p c t", p=128)), writes=[bf])
            bufs.append(bf)
        return (u.ap, bufs)

    def ffn_gateup(self, st, T, uT, wg, wu):
        nc, S = self.nc, self.S
        DC, FC = self.DC, self.FC
        uT_ap, uT_buf = uT
        tiles = self.tiles(T)
        sgr = Ring([Slot(S, nc, f"fsg{i}", [128, 512], F32, st) for i in range(2)])
        hst = Ring([Slot(S, nc, f"fh{i}", [128, T], BF16, st) for i in range(2)])
        for j in range(FC):
            sg_w = self.wtile(wg, 0, DC, j * 128, 128)
            su_w = self.wtile(wu, 0, DC, j * 128, 128)
            hs = hst.next()
            for (t0, n) in tiles:
                psA, pbA = self.bank()
                self.mm_group(psA, pbA, 128, n, [sg_w.ap[:, c, :] for c in range(DC)],
                              [uT_ap[:, c, t0:t0 + n] for c in range(DC)], [sg_w.buf, uT_buf[t0 // 512]])
                sg = sgr.next()
                S.op(S.act, lambda: nc.scalar.activation(out=sg.ap[:, 0:n], in_=psA[:, 0:n], func=AF.Silu), reads=[pbA], writes=[sg.buf])
                psB, pbB = self.bank()
                self.mm_group(psB, pbB, 128, n, [su_w.ap[:, c, :] for c in range(DC)],
                              [uT_ap[:, c, t0:t0 + n] for c in range(DC)], [su_w.buf, uT_buf[t0 // 512]])
                S.op(S.dve, lambda: nc.vector.tensor_tensor(out=hs.ap[:, t0:t0 + n], in0=sg.ap[:, 0:n], in1=psB[:, 0:n], op=ALU.mult),
                     reads=[sg.buf, pbB], writes=[hs.buf])
            S.dma(S.sp, hs.ch(), lambda: nc.sync.dma_start(out=self.hidT[j * 128:(j + 1) * 128, 0:T], in_=hs.ap[:, 0:T]),
                  reads=[hs.buf], writes=[self.dbuf(("hid", j))])

    def ssq_accum(self, ssq, sqt, src, oc, t0, n):
        nc, S = self.nc, self.S
        if oc == 0:
            S.op(S.act, lambda: nc.scalar.activation(out=ssq.ap[:, t0:t0 + n], in_=src.ap[:, 0:n], func=AF.Square), reads=[src.buf], writes=[ssq.buf])
        else:
            t = sqt.next()
            S.op(S.act, lambda: nc.scalar.activation(out=t.ap[:, 0:n], in_=src.ap[:, 0:n], func=AF.Square), reads=[src.buf], writes=[t.buf])
            S.op(S.dve, lambda: nc.vector.tensor_tensor(out=ssq.ap[:, t0:t0 + n], in0=ssq.ap[:, t0:t0 + n], in1=t.ap[:, 0:n], op=ALU.add),
                 reads=[ssq.buf, t.buf], writes=[ssq.buf])

    def ssq_store(self, ssq, T):
        nc, S = self.nc, self.S
        S.dma(S.sp, ssq.ch(), lambda: nc.sync.dma_start(out=self.ssqd[:, 0:T], in_=ssq.ap[:, 0:T]), reads=[ssq.buf])

    def ffn_down(self, st, T, wd):
        nc, S = self.nc, self.S
        DC, FC = self.DC, self.FC
        tiles = self.tiles(T)
        NQ = 4
        bounds = [(FC * q) // NQ for q in range(NQ + 1)]
        nqmax = max(bounds[q + 1] - bounds[q] for q in range(NQ))
        hq = Slot(S, nc, "dhq", [128, nqmax, T], BF16, st)
        hq_buf = [Buf(f"hq{i}") for i in range(len(tiles))]
        hq_ch = [S.dma_chan(f"c_hq{i}", persistent=False) for i in range(len(tiles))]
        yin = Ring([Slot(S, nc, f"dyi{i}", [128, 512], F32, st) for i in range(3)])
        yout = Ring([Slot(S, nc, f"dyo{i}", [128, 512], F32, st) for i in range(3)])
        ssq = Slot(S, nc, "dssq", [128, T], F32, st)
        sqt = Ring([Slot(S, nc, f"dsq{i}", [128, 512], F32, st) for i in range(2)])
        for q in range(NQ):
            k0, k1 = bounds[q], bounds[q + 1]
            nq = k1 - k0
            for ti, (t0, n) in enumerate(tiles):
                S.dma(S.sp, hq_ch[ti], lambda: nc.sync.dma_start(out=hq.ap[:, 0:nq, t0:t0 + n],
                                                                   in_=self.hidT[k0 * 128:k1 * 128, t0:t0 + n].rearrange("(c p) t -> p c t", p=128)),
                      writes=[hq_buf[ti]])
            for oc in range(DC):
                ws = self.wtile(wd, k0, nq, oc * 128, 128)
                for ti, (t0, n) in enumerate(tiles):
                    ps, pb = self.bank()
                    self.mm_group(ps, pb, 128, n, [ws.ap[:, c, :] for c in range(nq)],
                                  [hq.ap[:, c, t0:t0 + n] for c in range(nq)], [ws.buf, hq_buf[ti]])
                    yb = self.dbuf(("y", oc, ti))
                    yo = yout.next()
                    if q == 0:
                        S.op(S.act, lambda: nc.scalar.copy(out=yo.ap[:, 0:n], in_=ps[:, 0:n]), reads=[pb], writes=[yo.buf])
                    else:
                        yi = yin.next()
                        S.dma(S.sp, yi.ch(), lambda: nc.sync.dma_start(out=yi.ap[:, 0:n], in_=self.yT[oc * 128:(oc + 1) * 128, t0:t0 + n]),
                              reads=[yb], writes=[yi.buf])
                        S.op(S.dve, lambda: nc.vector.tensor_tensor(out=yo.ap[:, 0:n], in0=ps[:, 0:n], in1=yi.ap[:, 0:n], op=ALU.add),
                             reads=[pb, yi.buf], writes=[yo.buf])
                    S.dma(S.act, yo.ch(), lambda: nc.scalar.dma_start(out=self.yT[oc * 128:(oc + 1) * 128, t0:t0 + n], in_=yo.ap[:, 0:n]),
                          reads=[yo.buf], writes=[yb])
                    if q == NQ - 1:
                        self.ssq_accum(ssq, sqt, yo, oc, t0, n)
        self.ssq_store(ssq, T)

    def proj(self, st, T, uT, own):
        nc, S = self.nc, self.S
        DC, FH, DH, FW, DQ = self.DC, self.FH, self.DH, self.FW, self.DQ
        uT_ap, uT_buf = uT
        win = self.w["w_in"]
        tiles = self.tiles(T)
        koff = self.TO if own else 0
        ost = Ring([Slot(S, nc, f"po{i}", [128, T], BF16, st) for i in range(2)])

        def fm(c0, nchunks, dst, dst_key, tcol0, func=None):
            for j in range(nchunks):
                ws = self.wtile(win, 0, DC, c0 + j * 128, 128)
                o = ost.next()
                for (t0, n) in tiles:
                    ps, pb = self.bank()
                    self.mm_group(ps, pb, 128, n, [ws.ap[:, c, :] for c in range(DC)],
                                  [uT_ap[:, c, t0:t0 + n] for c in range(DC)], [ws.buf, uT_buf[t0 // 512]])
                    if func is None:
                        eng = S.act if (self.ps_i % 2 == 0) else S.dve
                        if eng is S.act:
                            S.op(S.act, lambda: nc.scalar.copy(out=o.ap[:, t0:t0 + n], in_=ps[:, 0:n]), reads=[pb], writes=[o.buf])
                        else:
                            S.op(S.dve, lambda: nc.vector.tensor_copy(out=o.ap[:, t0:t0 + n], in_=ps[:, 0:n]), reads=[pb], writes=[o.buf])
                    else:
                        S.op(S.act, lambda: nc.scalar.activation(out=o.ap[:, t0:t0 + n], in_=ps[:, 0:n], func=func), reads=[pb], writes=[o.buf])
                S.dma(S.sp, o.ch(), lambda: nc.sync.dma_start(out=dst[j * 128:(j + 1) * 128, tcol0:tcol0 + T], in_=o.ap[:, 0:T]),
                      reads=[o.buf], writes=[self.dbuf((dst_key, j))])

        fm(self.c_fk, FW // 128, self.KfT, "KfT" + str(own), koff)
        fm(self.c_dk, DQ // 128, self.KdT, "KdT" + str(own), koff)
        if own:
            fm(self.c_fq, FW // 128, self.QfT, "QfT", 0)
            fm(self.c_dq, DQ // 128, self.QdT, "QdT", 0)
            fm(self.c_gf, 2 * self.D // 128, self.SG, "SG", 0, func=AF.Sigmoid)
        ws = self.wtile(win, 0, DC, self.c_fl, FH)
        zs = Slot(S, nc, "pz", [FH, T], F32, st)
        for (t0, n) in tiles:
            ps, pb = self.bank()
            self.mm_group(ps, pb, FH, n, [ws.ap[:, c, 0:FH] for c in range(DC)],
                          [uT_ap[:, c, t0:t0 + n] for c in range(DC)], [ws.buf, uT_buf[t0 // 512]])
            S.op(S.act, lambda: nc.scalar.copy(out=zs.ap[:, t0:t0 + n], in_=ps[:FH, 0:n]), reads=[pb], writes=[zs.buf])
        S.dma(S.sp, zs.ch(), lambda: nc.sync.dma_start(out=self.zT[:, koff:koff + T], in_=zs.ap[:, 0:T]), reads=[zs.buf],
              writes=[self.dbuf(("zT", own))])
        nvc = (FW + DQ) // 128
        tb = [(t, min(128, T - t)) for t in range(0, T, 128)]
        vst = Ring([Slot(S, nc, f"pv{i}", [128, 4, 128], BF16, st) for i in range(3)])
        kb0 = (self.MKB + 1) if own else 0
        for j in range(nvc):
            c0 = (self.c_fv + j * 128) if j < FW // 128 else (self.c_dv + (j - FW // 128) * 128)
            ws = self.wtile(win, 0, DC, c0, 128)
            for g0 in range(0, len(tb), 4):
                grp = tb[g0:g0 + 4]
                ps, pb = self.bank()
                vs = vst.next()
                fns = []
                for gi, (t0, n) in enumerate(grp):
                    for c in range(DC):
                        fns.append(lambda gi=gi, t0=t0, n=n, c=c: nc.tensor.matmul(
                            ps[:n, gi * 128:(gi + 1) * 128], lhsT=uT_ap[:, c, t0:t0 + n], rhs=ws.ap[:, c, :],
                            start=(c == 0), stop=(c == DC - 1)))
                S.group(S.pe, fns, reads=[ws.buf, uT_buf[grp[0][0] // 512]], writes=[pb])
                nfull = sum(1 for (_, n) in grp if n == 128)
                if nfull:
                    S.op(S.dve, lambda: nc.vector.tensor_copy(out=vs.ap[:, 0:nfull, :], in_=ps[:, 0:nfull * 128].rearrange("p (g f) -> p g f", f=128)),
                         reads=[pb], writes=[vs.buf])
                    kb = kb0 + g0
                    S.dma(S.sp, vs.ch(), lambda: nc.sync.dma_start(
                        out=self.Vtok[kb * 128:(kb + nfull) * 128, j * 128:(j + 1) * 128].rearrange("(g p) f -> p g f", p=128),
                        in_=vs.ap[:, 0:nfull, :]), reads=[vs.buf], writes=[self.dbuf(("V", own, j))])
                if nfull < len(grp):
                    gi = nfull
                    n = grp[gi][1]
                    vs2 = vst.next()
                    S.op(S.dve, lambda: nc.vector.tensor_copy(out=vs2.ap[:n, 0, :], in_=ps[:n, gi * 128:(gi + 1) * 128]), reads=[pb], writes=[vs2.buf])
                    kb = kb0 + g0 + gi
                    S.dma(S.sp, vs2.ch(), lambda: nc.sync.dma_start(out=self.Vtok[kb * 128:kb * 128 + n, j * 128:(j + 1) * 128], in_=vs2.ap[:n, 0, :]),
                          reads=[vs2.buf], writes=[self.dbuf(("V", own, j))])

    def attn_prep(self, st_out, st):
        nc, S = self.nc, self.S
        FH, TQ, TO, NM, LK, NKB, NQB, MKB = self.FH, self.TQ, self.TO, self.NM, self.LK, self.NKB, self.NQB, self.MKB
        cb = self.cbuf
        refB = Slot(S, nc, "arefB", [128, FH, NQB], F32, st_out)
        biask = Slot(S, nc, "abk", [128, NKB, FH], F32, st_out)
        dpos = Slot(S, nc, "adpos", [128, NKB, NQB], F32, st_out)
        z = Slot(S, nc, "az", [FH, LK], F32, st)
        S.dma(S.sp, z.ch(), lambda: nc.sync.dma_start(out=z.ap, in_=self.zT), reads=[self.dbuf(("zT", False)), self.dbuf(("zT", True))], writes=[z.buf])
        a = Slot(S, nc, "aa", [FH, LK], F32, st)
        m = Slot(S, nc, "am", [FH, LK], F32, st)
        zb = [z.buf, a.buf, m.buf]
        S.op(S.dve, lambda: nc.vector.tensor_scalar(out=z.ap, in0=z.ap, scalar1=self.cst["bfg"][:, 0:1], scalar2=None, op0=ALU.add), reads=[z.buf, cb], writes=[z.buf])
        S.op(S.act, lambda: nc.scalar.activation(out=a.ap, in_=z.ap, func=AF.Abs), reads=[z.buf], writes=[a.buf])
        S.op(S.act, lambda: nc.scalar.activation(out=a.ap, in_=a.ap, func=AF.Exp, scale=-1.0), reads=[a.buf], writes=[a.buf])
        S.op(S.act, lambda: nc.scalar.activation(out=a.ap, in_=a.ap, func=AF.Ln, bias=self.one_ap[:FH, :], scale=1.0), reads=[a.buf, cb], writes=[a.buf])
        S.op(S.dve, lambda: nc.vector.tensor_scalar(out=m.ap, in0=z.ap, scalar1=0.0, scalar2=None, op0=ALU.min), reads=[z.buf], writes=[m.buf])
        S.op(S.dve, lambda: nc.vector.tensor_tensor(out=m.ap, in0=m.ap, in1=a.ap, op=ALU.subtract), reads=[m.buf, a.buf], writes=[m.buf])
        onesf = Slot(S, nc, "aones", [FH, TQ], F32, st)
        S.op(S.dve, lambda: nc.vector.memset(onesf.ap, 1.0), writes=[onesf.buf])
        for (s0, s1) in ((0, TQ), (TQ, TO), (TO, LK)):
            S.op(S.dve, lambda s0=s0, s1=s1: nc.vector.tensor_tensor_scan(out=z.ap[:, s0:s1], data0=onesf.ap[:, 0:s1 - s0], data1=m.ap[:, s0:s1],
                                                                           initial=0.0, op0=ALU.mult, op1=ALU.add),
                 reads=[m.buf, onesf.buf], writes=[z.buf])
        off = Slot(S, nc, "aoff", [FH, 2], F32, st)
        S.op(S.dve, lambda: nc.vector.tensor_copy(out=off.ap[:, 0:1], in_=z.ap[:, TO - 1:TO]), reads=[z.buf], writes=[off.buf])
        S.op(S.dve, lambda: nc.vector.scalar_tensor_tensor(out=off.ap[:, 1:2], in0=z.ap[:, TQ - 1:TQ], scalar=self.cst["apf"][:, 0:1], in1=off.ap[:, 0:1],
                                                           op0=ALU.mult, op1=ALU.add), reads=[z.buf, off.buf, cb], writes=[off.buf])
        S.op(S.dve, lambda: nc.vector.tensor_scalar(out=z.ap[:, 0:TQ], in0=z.ap[:, 0:TQ], scalar1=off.ap[:, 0:1], scalar2=None, op0=ALU.add), reads=[z.buf, off.buf], writes=[z.buf])
        S.op(S.dve, lambda: nc.vector.tensor_scalar(out=z.ap[:, TO:LK], in0=z.ap[:, TO:LK], scalar1=off.ap[:, 1:2], scalar2=None, op0=ALU.add), reads=[z.buf, off.buf], writes=[z.buf])
        ctok = Slot(S, nc, "actok", [128, NKB, FH], F32, st)
        S.op(S.dve, lambda: nc.vector.memset(ctok.ap, 0.0), writes=[ctok.buf])
        ident = self.cst["ident"]
        for kb in range(NKB):
            if kb < MKB:
                c0, n = kb * 128, 128
            elif kb == MKB:
                c0, n = TQ, NM
            else:
                c0, n = TO + (kb - MKB - 1) * 128, 128
            ps, pb = self.bank()
            S.group(S.pe, [lambda: nc.tensor.transpose(out=ps[:n, 0:FH], in_=z.ap[:, c0:c0 + n], identity=ident)], reads=[z.buf, cb], writes=[pb])
            S.op(S.dve, lambda: nc.vector.tensor_copy(out=ctok.ap[:n, kb, :], in_=ps[:n, 0:FH]), reads=[pb], writes=[ctok.buf])
        ps, pb = self.bank()
        zown_first = z.ap[:, TO:LK].rearrange("h (q i) -> h q i", i=128)[:, :, 0]
        cf = Slot(S, nc, "acf", [FH, NQB], F32, st)
        S.op(S.dve, lambda: nc.vector.tensor_copy(out=cf.ap, in_=zown_first), reads=[z.buf], writes=[cf.buf])
        selS = Slot(S, nc, "asel", [FH, FH * 128], F32, st)
        S.dma(S.sp, selS.ch(), lambda: nc.sync.dma_start(out=selS.ap, in_=self.sel_d), writes=[selS.buf])
        sel = selS.ap
        fns = [lambda h=h: nc.tensor.matmul(ps[:, h * NQB:(h + 1) * NQB], lhsT=sel[:, h * 128:(h + 1) * 128], rhs=cf.ap, start=True, stop=True)
               for h in range(FH)]
        S.group(S.pe, fns, reads=[cf.buf, selS.buf], writes=[pb])
        S.op(S.dve, lambda: nc.vector.tensor_copy(out=refB.ap, in_=ps[:, 0:FH * NQB].rearrange("p (h q) -> p h q", q=NQB)), reads=[pb], writes=[refB.buf])
        S.op(S.dve, lambda: nc.vector.tensor_tensor(out=biask.ap, in0=self.cst["exclk"].unsqueeze(2).to_broadcast([128, NKB, FH]), in1=ctok.ap, op=ALU.subtract),
             reads=[ctok.buf, cb], writes=[biask.buf])
        S.op(S.dve, lambda: nc.vector.tensor_tensor(out=dpos.ap, in0=self.cst["kpos"].unsqueeze(2).to_broadcast([128, NKB, NQB]),
                                                    in1=self.cst["qpos0"].unsqueeze(1).to_broadcast([128, NKB, NQB]), op=ALU.subtract),
             reads=[cb], writes=[dpos.buf])
        return refB, biask, dpos

    def attention(self, st):
        nc, S = self.nc, self.S
        FH, DH, TQ, TO, NM, LK, NKB, NQB, MKB, FW = self.FH, self.DH, self.TQ, self.TO, self.NM, self.LK, self.NKB, self.NQB, self.MKB, self.FW
        cb = self.cbuf
        with contextlib.ExitStack() as st_tmp:
            refB, biask, dpos = self.attn_prep(st, st_tmp)
            S.barrier()
        scale = 1.0 / math.sqrt(128.0)
        ones = self.cst["ones"]
        tri = self.cst["tri"]
        NQT = TQ // 512
        kts = Ring([Slot(S, nc, f"tk{i}", [128, LK], BF16, st) for i in range(4)])
        qts = Ring([Slot(S, nc, f"tq{i}", [128, TQ], BF16, st) for i in range(4)])
        vts = Ring([Slot(S, nc, f"tv{i}", [128, NKB, 256], BF16, st) for i in range(2)])
        bms = Ring([Slot(S, nc, f"tb{i}", [128, NKB, NQB], F32, st) for i in range(2)])
        pts = Ring([Slot(S, nc, f"tp{i}", [128, 512], BF16, st) for i in range(8)])
        rinv = Slot(S, nc, "trinv", [128, 512], F32, st)
        racc = Ring([Slot(S, nc, f"tra{i}", [128, 512], F32, st) for i in range(2)])
        evs = Ring([Slot(S, nc, f"tev{i}", [128, 3, 512], F32, st) for i in range(2)])
        ost = Ring([Slot(S, nc, f"to{i}", [128, 512], BF16, st) for i in range(2)])
        t1 = Slot(S, nc, "tt1", [128, 2, 512], F32, st)
        t2 = Slot(S, nc, "tt2", [128, 2, 512], F32, st)
        sq = Slot(S, nc, "tsq", [128, 2, 512], BF16, st)
        rsd = Slot(S, nc, "trsd", [128, 512], F32, st)

        def load_v(vt, col0, ncol):
            segs = [(0, MKB, 0), (MKB + 1, NKB, MKB + 1)]
            for (b0, b1, _) in segs:
                S.dma(S.sp, vt.ch(), lambda b0=b0, b1=b1: nc.sync.dma_start(
                    out=vt.ap[:, b0:b1, 0:ncol], in_=self.Vtok[b0 * 128:b1 * 128, col0:col0 + ncol].rearrange("(g p) f -> p g f", p=128)),
                    reads=[self.dbuf(("V", o, j)) for o in (False, True) for j in range((col0) // 128, (col0 + ncol) // 128)], writes=[vt.buf])
            S.dma(S.sp, vt.ch(), lambda: nc.sync.dma_start(out=vt.ap[:NM, MKB, 0:ncol], in_=self.Vtok[MKB * 128:MKB * 128 + NM, col0:col0 + ncol]),
                  reads=[self.dbuf(("V", False, j)) for j in range((col0) // 128, (col0 + ncol) // 128)], writes=[vt.buf])

        def kblocks(qt):
            lst = [(kb, 128, 0) for kb in range(MKB)]
            lst.append((MKB, NM, 0))
            for kbo in range(4 * qt + 4):
                r = kbo - 4 * qt
                lst.append((MKB + 1 + kbo, 128, 128 * r if r > 0 else 0))
            return lst

        sbanks = [self.psum[i] for i in (0, 1, 2, 3, 4)]
        abanks = [[self.psum[i] for i in (5, 6, 7)]]
        cnt = {"s": 0, "a": 0}
        LOOK = 3

        def sbank():
            cnt["s"] += 1
            return sbanks[cnt["s"] % len(sbanks)]

        def aset():
            cnt["a"] += 1
            return abanks[cnt["a"] % len(abanks)]

        def head_pass(kt, qs, vt, ndv, bm, qt, ps_o, ps_r, wsub=128):
            kl = kblocks(qt)
            nb = len(kl)

            def qk_fn(idx):
                kb, rows, c0 = kl[idx]
                kc0 = TQ if kb == MKB else (kb * 128 if kb < MKB else TO + (kb - MKB - 1) * 128)
                ps, pb = sbank()
                fn = lambda: nc.tensor.matmul(ps[:rows, :512], lhsT=kt.ap[:, kc0:kc0 + rows], rhs=qs.ap[:, qt * 512:(qt + 1) * 512], start=True, stop=True)
                return fn, (ps, pb)

            def qk(idx):
                fn, (ps, pb) = qk_fn(idx)
                S.group(S.pe, [fn], reads=[kt.buf, qs.buf], writes=[pb])
                return ps, pb

            ra = racc.next()
            S.op(S.dve, lambda: nc.vector.memset(ra.ap, 0.0), writes=[ra.buf])
            pend = [qk(i) for i in range(min(LOOK, nb))]
            for idx, (kb, rows, c0) in enumerate(kl):
                first, last = idx == 0, idx == nb - 1
                qk_extra = None
                if idx + LOOK < nb:
                    qk_extra = qk_fn(idx + LOOK)
                    pend.append(qk_extra[1])
                ps, pb = pend.pop(0)
                p = pts.next()
                kbo = kb - MKB - 1
                segs = []
                for a0 in range(0, 512, wsub):
                    lo, hi = max(a0, c0), a0 + wsub
                    if lo < hi:
                        segs.append((lo, hi, qt * 4 + a0 // 128))
                fns = [lambda lo=lo, hi=hi, bc=bc: nc.scalar.activation(out=p.ap[:rows, lo:hi], in_=ps[:rows, lo:hi], func=AF.Exp,
                                                                         bias=bm.ap[:rows, kb, bc:bc + 1], scale=scale)
                       for (lo, hi, bc) in segs]
                S.group(S.act, fns, reads=[pb, bm.buf], writes=[p.buf])
                if kb > MKB and kbo >= 4 * qt:
                    j = kbo - 4 * qt
                    S.op(S.pool, lambda j=j: nc.gpsimd.tensor_tensor(out=p.ap[:, j * 128:(j + 1) * 128], in0=p.ap[:, j * 128:(j + 1) * 128], in1=tri, op=ALU.mult),
                         reads=[p.buf, cb], writes=[p.buf])
                fns = []
                for i in range(ndv):
                    fns.append(lambda i=i: nc.tensor.matmul(ps_o[i][0][:, c0:512], lhsT=vt.ap[:rows, kb, i * 128:(i + 1) * 128], rhs=p.ap[:rows, c0:512],
                                                            start=first, stop=last))
                if qk_extra is not None:
                    S.group(S.pe, [qk_extra[0]] + fns, reads=[p.buf, vt.buf, cb, kt.buf, qs.buf], writes=[ps_o[i][1] for i in range(ndv)] + [qk_extra[1][1]])
                else:
                    S.group(S.pe, fns, reads=[p.buf, vt.buf, cb], writes=[ps_o[i][1] for i in range(ndv)])
                S.op(S.dve, lambda: nc.vector.tensor_tensor(out=ra.ap[:rows, c0:512], in0=ra.ap[:rows, c0:512], in1=p.ap[:rows, c0:512], op=ALU.add),
                     reads=[ra.buf, p.buf], writes=[ra.buf])
            S.group(S.pe, [lambda: nc.tensor.matmul(ps_r[0], lhsT=self.cst["ones32"], rhs=ra.ap, start=True, stop=True)], reads=[ra.buf, cb], writes=[ps_r[1]])

        sub2 = self.cst["subg2"]

        def fox_load(h):
            kt, qs, vt, bm = kts.next(), qts.next(), vts.next(), bms.next()
            S.dma(S.sp, kt.ch(), lambda: nc.sync.dma_start(out=kt.ap, in_=self.KfT[h * 128:(h + 1) * 128, :]), writes=[kt.buf])
            S.dma(S.sp, qs.ch(), lambda: nc.sync.dma_start(out=qs.ap, in_=self.QfT[h * 128:(h + 1) * 128, :]), writes=[qs.buf])
            load_v(vt, h * 128, 128)
            S.op(S.dve, lambda: nc.vector.tensor_tensor(out=bm.ap, in0=biask.ap[:, :, h].unsqueeze(2).to_broadcast([128, NKB, NQB]),
                                                        in1=refB.ap[:, h, :].unsqueeze(1).to_broadcast([128, NKB, NQB]), op=ALU.add),
                 reads=[biask.buf, refB.buf], writes=[bm.buf])
            return (kt, qs, vt, bm)

        def fox_compute(h, res):
            kt, qs, vt, bm = res
            for qt in range(NQT):
                bs = aset()
                ps_o = [bs[0]]
                ps_r = bs[2]
                head_pass(kt, qs, vt, 1, bm, qt, ps_o, ps_r)
                ev = evs.next()
                S.op(S.dve, lambda: nc.vector.tensor_copy(out=ev.ap[:, 0, :], in_=ps_o[0][0]), reads=[ps_o[0][1]], writes=[ev.buf])
                S.op(S.dve, lambda: nc.vector.reciprocal(out=rinv.ap, in_=ps_r[0]), reads=[ps_r[1]], writes=[rinv.buf])
                o = ost.next()
                S.op(S.dve, lambda: nc.vector.tensor_tensor(out=o.ap, in0=ev.ap[:, 0, :], in1=rinv.ap, op=ALU.mult), reads=[ev.buf, rinv.buf], writes=[o.buf])
                S.dma(S.pool, o.ch(), lambda: nc.gpsimd.dma_start(out=self.attT[h * 128:(h + 1) * 128, qt * 512:(qt + 1) * 512], in_=o.ap), reads=[o.buf])

        def diff_load(h):
            vt = vts.next()
            load_v(vt, FW + h * 256, 256)
            bm = bms.next()
            slope = 2.0 ** (-8.0 * (h + 1) / DH)
            S.op(S.dve, lambda: nc.vector.tensor_scalar(out=bm.ap, in0=dpos.ap, scalar1=float(slope), scalar2=None, op0=ALU.mult), reads=[dpos.buf], writes=[bm.buf])
            comps = []
            for c in range(2):
                kt, qs = kts.next(), qts.next()
                r0 = (h * 2 + c) * 128
                S.dma(S.sp, kt.ch(), lambda: nc.sync.dma_start(out=kt.ap, in_=self.KdT[r0:r0 + 128, :]), writes=[kt.buf])
                S.dma(S.sp, qs.ch(), lambda: nc.sync.dma_start(out=qs.ap, in_=self.QdT[r0:r0 + 128, :]), writes=[qs.buf])
                comps.append((kt, qs))
            return (vt, bm, comps)

        def diff_compute(h, res):
            vt, bm, comps = res
            slope = 2.0 ** (-8.0 * (h + 1) / DH)
            wsub = 512 if slope * 511 <= 64.0 else (256 if slope * 255 <= 64.0 else 128)
            for qt in range(NQT):
                for c in range(2):
                    kt, qs = comps[c]
                    bs = aset()
                    ps_o = [bs[0], bs[1]]
                    ps_r = bs[2]
                    head_pass(kt, qs, vt, 2, bm, qt, ps_o, ps_r, wsub=wsub)
                    ev = evs.next()
                    for i in range(2):
                        S.op(S.dve, lambda i=i: nc.vector.tensor_copy(out=ev.ap[:, i, :], in_=ps_o[i][0]), reads=[ps_o[i][1]], writes=[ev.buf])
                    S.op(S.dve, lambda: nc.vector.reciprocal(out=rinv.ap, in_=ps_r[0]), reads=[ps_r[1]], writes=[rinv.buf])
                    tt = t1 if c == 0 else t2
                    for i in range(2):
                        S.op(S.dve, lambda i=i: nc.vector.tensor_tensor(out=tt.ap[:, i, :], in0=ev.ap[:, i, :], in1=rinv.ap, op=ALU.mult),
                             reads=[ev.buf, rinv.buf], writes=[tt.buf])
                S.op(S.dve, lambda: nc.vector.scalar_tensor_tensor(out=t1.ap, in0=t2.ap, scalar=self.neg_lam, in1=t1.ap, op0=ALU.mult, op1=ALU.add),
                     reads=[t1.buf, t2.buf, cb], writes=[t1.buf])
                S.op(S.act, lambda: nc.scalar.activation(out=sq.ap, in_=t1.ap, func=AF.Square), reads=[t1.buf], writes=[sq.buf])
                ps, pb = sbank()
                self.mm_group(ps, pb, 128, 512, [ones, ones], [sq.ap[:, 0, :], sq.ap[:, 1, :]], [sq.buf, cb])
                S.op(S.act, lambda: nc.scalar.activation(out=rsd.ap, in_=ps, func=AF.Sqrt, scale=1.0 / 256.0, bias=self.eps_ap), reads=[pb, cb], writes=[rsd.buf])
                S.op(S.dve, lambda: nc.vector.reciprocal(out=rsd.ap, in_=rsd.ap), reads=[rsd.buf], writes=[rsd.buf])
                for i in range(2):
                    o = ost.next()
                    S.op(S.dve, lambda i=i: nc.vector.scalar_tensor_tensor(out=o.ap, in0=t1.ap[:, i, :], scalar=sub2[:, i:i + 1], in1=rsd.ap, op0=ALU.mult, op1=ALU.mult),
                         reads=[t1.buf, rsd.buf, cb], writes=[o.buf])
                    r0 = FW + h * 256 + i * 128
                    S.dma(S.pool, o.ch(), lambda: nc.gpsimd.dma_start(out=self.attT[r0:r0 + 128, qt * 512:(qt + 1) * 512], in_=o.ap), reads=[o.buf])

        jobs = [(fox_load, fox_compute, h) for h in range(FH)] + [(diff_load, diff_compute, h) for h in range(DH)]
        res = jobs[0][0](jobs[0][2])
        for ji, (ld, cp, h) in enumerate(jobs):
            nres = jobs[ji + 1][0](jobs[ji + 1][2]) if ji + 1 < len(jobs) else None
            cp(h, res)
            res = nres

    def merge(self, st):
        nc, S = self.nc, self.S
        DC, AC, TQ, FW, D = self.DC, self.AC, self.TQ, self.FW, self.D
        att_ap, att_bufs = self.load_tiled(st, "matt", self.attT, AC, TQ)
        nf = FW // 128
        nd = AC - nf
        gst = Ring([Slot(S, nc, f"mg{i}", [128, 2, 512], BF16, st) for i in range(3)])
        m1 = Ring([Slot(S, nc, f"mm{i}", [128, 512], F32, st) for i in range(2)])
        m2 = Ring([Slot(S, nc, f"mn{i}", [128, 512], F32, st) for i in range(2)])
        ost = Ring([Slot(S, nc, f"mo{i}", [128, 512], BF16, st) for i in range(3)])
        for oc in range(DC):
            wf = self.wtile(self.w["w_o_fox"], 0, nf, oc * 128, 128)
            wd = self.wtile(self.w["w_o_diff"], 0, nd, oc * 128, 128)
            for (t0, n) in self.tiles(TQ):
                g = gst.next()
                for i in range(2):
                    S.dma(S.sp, g.ch(), lambda i=i: nc.sync.dma_start(out=g.ap[:, i, 0:n], in_=self.SG[i * D + oc * 128:i * D + (oc + 1) * 128, t0:t0 + n]),
                          writes=[g.buf])
                psA, pbA = self.bank()
                self.mm_group(psA, pbA, 128, n, [wf.ap[:, c, :] for c in range(nf)], [att_ap[:, c, t0:t0 + n] for c in range(nf)], [wf.buf, att_bufs[t0 // 512]])
                psB, pbB = self.bank()
                self.mm_group(psB, pbB, 128, n, [wd.ap[:, c, :] for c in range(nd)], [att_ap[:, nf + c, t0:t0 + n] for c in range(nd)], [wd.buf, att_bufs[t0 // 512]])
                a, b = m1.next(), m2.next()
                o = ost.next()
                S.op(S.dve, lambda: nc.vector.tensor_tensor(out=a.ap[:, 0:n], in0=psA[:, 0:n], in1=g.ap[:, 0, 0:n], op=ALU.mult), reads=[pbA, g.buf], writes=[a.buf])
                S.op(S.dve, lambda: nc.vector.tensor_tensor(out=b.ap[:, 0:n], in0=psB[:, 0:n], in1=g.ap[:, 1, 0:n], op=ALU.mult), reads=[pbB, g.buf], writes=[b.buf])
                S.op(S.dve, lambda: nc.vector.tensor_tensor(out=o.ap[:, 0:n], in0=a.ap[:, 0:n], in1=b.ap[:, 0:n], op=ALU.add), reads=[a.buf, b.buf], writes=[o.buf])
                S.dma(S.sp, o.ch(), lambda: nc.sync.dma_start(out=self.mrgT[oc * 128:(oc + 1) * 128, t0:t0 + n], in_=o.ap[:, 0:n]), reads=[o.buf])

    def outproj(self, st):
        nc, S = self.nc, self.S
        DC, TQ = self.DC, self.TQ
        mg_ap, mg_bufs = self.load_tiled(st, "omg", self.mrgT, DC, TQ)
        ost = Ring([Slot(S, nc, f"oo{i}", [128, 512], F32, st) for i in range(3)])
        ssq = Slot(S, nc, "ossq", [128, TQ], F32, st)
        sqt = Ring([Slot(S, nc, f"osq{i}", [128, 512], F32, st) for i in range(2)])
        for oc in range(DC):
            ws = self.wtile(self.w["w_out"], 0, DC, oc * 128, 128)
            for ti, (t0, n) in enumerate(self.tiles(TQ)):
                ps, pb = self.bank()
                self.mm_group(ps, pb, 128, n, [ws.ap[:, c, :] for c in range(DC)], [mg_ap[:, c, t0:t0 + n] for c in range(DC)], [ws.buf, mg_bufs[t0 // 512]])
                o = ost.next()
                if (oc + ti) % 2 == 0:
                    S.op(S.act, lambda: nc.scalar.copy(out=o.ap[:, 0:n], in_=ps[:, 0:n]), reads=[pb], writes=[o.buf])
                else:
                    S.op(S.dve, lambda: nc.vector.tensor_copy(out=o.ap[:, 0:n], in_=ps[:, 0:n]), reads=[pb], writes=[o.buf])
                S.dma(S.sp, o.ch(), lambda: nc.sync.dma_start(out=self.mixT[oc * 128:(oc + 1) * 128, t0:t0 + n], in_=o.ap[:, 0:n]), reads=[o.buf],
                      writes=[self.dbuf(("mix", oc, ti))])
                self.ssq_accum(ssq, sqt, o, oc, t0, n)
        self.ssq_store(ssq, TQ)

    def build(self):
        nc, S = self.nc, self.S
        DC, TQ, TO = self.DC, self.TQ, self.TO
        self.load_consts()
        t = nc.sbuf_tensor("k_eps", [128, 2], F32).__enter__()
        e = t.ap() if callable(getattr(t, "ap", None)) else t
        S.op(S.dve, lambda: nc.vector.memset(e[:, 0:1], RMS_EPS), writes=[self.cbuf])
        S.op(S.dve, lambda: nc.vector.memset(e[:, 1:2], 1.0), writes=[self.cbuf])
        self.eps_ap = e[:, 0:1]
        self.one_ap = e[:, 1:2]
        G = self.gain
        for own in (False, True):
            T = TQ if own else TO
            xsrc = self.xT_own if own else self.xT_oth
            with contextlib.ExitStack() as st:
                self.norm_stage(st, T, xsrc, None, 0, None, None, G(0))
                S.barrier()
            with contextlib.ExitStack() as st:
                uT = self.load_u(st, T)
                self.ffn_gateup(st, T, uT, self.w["ff1_w_gate"], self.w["ff1_w_up"])
                S.barrier()
            with contextlib.ExitStack() as st:
                self.ffn_down(st, T, self.w["ff1_w_down"])
                S.barrier()
            with contextlib.ExitStack() as st:
                self.norm_stage(st, T, xsrc, self.yT, 0.5, G(1), self.h1T if own else None, G(2))
                S.barrier()
            with contextlib.ExitStack() as st:
                uT = self.load_u(st, T)
                self.proj(st, T, uT, own)
                S.barrier()
        with contextlib.ExitStack() as st:
            self.attention(st)
            S.barrier()
        with contextlib.ExitStack() as st:
            self.merge(st)
            S.barrier()
        with contextlib.ExitStack() as st:
            self.outproj(st)
            S.barrier()
        with contextlib.ExitStack() as st:
            self.norm_stage(st, TQ, self.h1T, self.mixT, 1.0, G(3), self.h2T, G(4))
            S.barrier()
        with contextlib.ExitStack() as st:
            uT = self.load_u(st, TQ)
            self.ffn_gateup(st, TQ, uT, self.w["ff2_w_gate"], self.w["ff2_w_up"])
            S.barrier()
        with contextlib.ExitStack() as st:
            self.ffn_down(st, TQ, self.w["ff2_w_down"])
            S.barrier()
        with contextlib.ExitStack() as st:
            self.norm_stage(st, TQ, self.h2T, self.yT, 0.5, G(5), self.outT, None)
            S.barrier()
        return nc


def host_inputs(cfg, inp):
    D, DFF, TQ, NM, FH, DH, B = (cfg[k] for k in ("D", "DFF", "TQ", "NM", "FH", "DH", "B"))
    DC = D // 128
    NQB = TQ // 128
    NKB = 2 * NQB + 1
    f32 = np.float32
    x = np.asarray(inp["x"], f32)
    meta = np.asarray(inp["meta_tokens"], f32)

    def pc(v):
        return np.ascontiguousarray(np.asarray(v, f32).reshape(-1, 128).T)

    gains = np.concatenate([pc(inp[k][0]) for k in ("ff1_pre_g", "ff1_post_g", "mix_pre_g", "mix_post_g", "ff2_pre_g", "ff2_post_g")], axis=1)
    lamv = np.concatenate([np.broadcast_to(np.asarray(inp[k][0], f32)[None, :], (128, 128)) for k in ("lambda_q1", "lambda_k1", "lambda_q2", "lambda_k2")], axis=1)
    shared = {
        "gains": np.ascontiguousarray(gains),
        "bfg": np.ascontiguousarray(np.asarray(inp["b_forget"][0], f32).reshape(FH, 1)),
        "lamv": np.ascontiguousarray(lamv),
        "subg": pc(inp["diff_subln_g"][0]),
        "tri": np.triu(np.ones((128, 128), f32)),
        "sel": np.ascontiguousarray(np.repeat(np.eye(FH, dtype=f32), 128, axis=1)),
        "ident": np.eye(FH, dtype=f32),
        "slopes": np.zeros((128, DH), f32),
    }
    for k in ("ff1_w_gate", "ff1_w_up", "ff1_w_down", "w_in", "w_o_fox", "w_o_diff", "w_out", "ff2_w_gate", "ff2_w_up", "ff2_w_down"):
        shared[k] = np.ascontiguousarray(np.asarray(inp[k][0], f32))
    maps = []
    metaT = np.ascontiguousarray(meta.T)
    for core in range(2 * B):
        b, p = core // 2, core % 2
        own = x[b, p * TQ:(p + 1) * TQ]
        oth = x[b, (1 - p) * TQ:(2 - p) * TQ]
        m = dict(shared)
        m["xT_own"] = np.ascontiguousarray(own.T)
        m["xT_oth"] = np.ascontiguousarray(np.concatenate([oth.T, metaT], axis=1))
        kpos = np.full((128, NKB), -1.0e9, f32)
        exclk = np.full((128, NKB), NEG, f32)
        i = np.arange(128, dtype=f32)
        for kb in range(NQB):
            if p == 1:
                kpos[:, kb] = NM + kb * 128 + i
                exclk[:, kb] = 0.0
            kpos[:, NQB + 1 + kb] = NM + p * TQ + kb * 128 + i
            exclk[:, NQB + 1 + kb] = 0.0
        kpos[:NM, NQB] = np.arange(NM)
        exclk[:NM, NQB] = 0.0
        m["kpos"] = kpos
        m["exclk"] = exclk
        m["qpos0"] = np.ascontiguousarray(np.broadcast_to((NM + p * TQ + 128 * np.arange(NQB, dtype=f32))[None, :], (128, NQB)))
        m["apf"] = np.full((FH, 1), float(p), f32)
        maps.append(m)
    return maps


_CACHE = {}


def run(cfg, inp):
    key = tuple(sorted(cfg.items()))
    if key not in _CACHE:
        _CACHE[key] = Builder(cfg).build()
    nc = _CACHE[key]
    maps = host_inputs(cfg, inp)
    ncores = 2 * cfg["B"]
    res = run_bass_kernel_spmd(nc, maps, core_ids=list(range(ncores)))
    TQ, D, B = cfg["TQ"], cfg["D"], cfg["B"]
    out = np.empty((B, 2 * TQ, D), np.float32)
    for core in range(ncores):
        b, p = core // 2, core % 2
        out[b, p * TQ:(p + 1) * TQ, :] = np.asarray(res.results[core]["outT"]).T
    return out


def kernel(**inputs):
    return run(FULL, inputs)
```

```python
import contextlib
import math
import numpy as np
import concourse.bass as bass
import concourse.mybir as mybir
from concourse.bass_utils import run_bass_kernel_spmd

F32 = mybir.dt.float32
BF16 = mybir.dt.bfloat16
AF = mybir.ActivationFunctionType
ALU = mybir.AluOpType

FULL = dict(D=4096, DFF=11008, TQ=2048, NM=16, FH=16, DH=8, B=4)
RMS_EPS = 1e-6
LAMBDA_INIT = 0.8 - 0.6 * math.exp(-0.3 * 0)
NEG = -30000.0


class Buf:
    __slots__ = ("name", "lw", "rd")

    def __init__(self, name):
        self.name = name
        self.lw = None
        self.rd = []


class Chan:
    def __init__(self, nc, name, step):
        self.sem = nc.alloc_semaphore(name)
        self.name = name
        self.step = step
        self.val = 0


class Eng:
    def __init__(self, nc, name, handle, self_sync):
        self.name = name
        self.h = handle
        self.chan = Chan(nc, "s_" + name, 1)
        self.waited = {}
        self.self_sync = self_sync


class Sched:
    def __init__(self, nc):
        self.nc = nc
        self.pe = Eng(nc, "pe", nc.tensor, False)
        self.act = Eng(nc, "act", nc.scalar, True)
        self.dve = Eng(nc, "dve", nc.vector, True)
        self.pool = Eng(nc, "pool", nc.gpsimd, True)
        self.sp = Eng(nc, "sp", nc.sync, True)
        self.engs = [self.pe, self.act, self.dve, self.pool, self.sp]
        self.chans = [e.chan for e in self.engs]
        self.free = []
        self.inphase = []

    def dma_chan(self, name, persistent=True):
        if not persistent and self.free:
            c = self.free.pop()
        else:
            c = Chan(self.nc, name, 16)
            self.chans.append(c)
        if not persistent:
            self.inphase.append(c)
        return c

    def release_phase(self):
        self.free.extend(self.inphase)
        self.inphase = []

    def _deps(self, reads, writes):
        deps = {}

        def add(d):
            if d is None:
                return
            c, v = d
            if deps.get(c, 0) < v:
                deps[c] = v
        for b in reads:
            add(b.lw)
        for b in writes:
            add(b.lw)
            for r in b.rd:
                add(r)
        return deps

    def _wait(self, eng, deps):
        for c, v in deps.items():
            if c is eng.chan and not eng.self_sync:
                continue
            if eng.waited.get(c, 0) >= v:
                continue
            eng.h.wait_ge(c.sem, v)
            eng.waited[c] = v

    def _mark(self, tok, reads, writes):
        for b in reads:
            b.rd.append(tok)
            if len(b.rd) > 24:
                best = {}
                for c, v in b.rd:
                    if best.get(c, 0) < v:
                        best[c] = v
                b.rd = list(best.items())
        for b in writes:
            b.lw = tok
            b.rd = []

    def op(self, eng, fn, reads=(), writes=()):
        self._wait(eng, self._deps(reads, writes))
        inst = fn()
        eng.chan.val += 1
        inst.then_inc(eng.chan.sem, 1)
        self._mark((eng.chan, eng.chan.val), reads, writes)

    def group(self, eng, fns, reads=(), writes=()):
        self._wait(eng, self._deps(reads, writes))
        inst = None
        for fn in fns:
            inst = fn()
        eng.chan.val += 1
        inst.then_inc(eng.chan.sem, 1)
        self._mark((eng.chan, eng.chan.val), reads, writes)

    def dma(self, eng, chan, fn, reads=(), writes=()):
        self._wait(eng, self._deps(reads, writes))
        inst = fn()
        chan.val += 16
        inst.then_inc(chan.sem, 16)
        self._mark((chan, chan.val), reads, writes)

    def barrier(self):
        for e in self.engs:
            for c in self.chans:
                if c.val == 0 or e.waited.get(c, 0) >= c.val:
                    continue
                e.h.wait_ge(c.sem, c.val)
                e.waited[c] = c.val
        self.release_phase()


class Slot:
    _uid = [0]

    def __init__(self, S, nc, name, shape, dtype, stack=None):
        Slot._uid[0] += 1
        name = f"{name}_{Slot._uid[0]}"
        if stack is not None:
            t = stack.enter_context(nc.sbuf_tensor(name, list(shape), dtype))
        else:
            t = nc.alloc_sbuf_tensor(name, list(shape), dtype)
        self.ap = t.ap() if callable(getattr(t, "ap", None)) else t
        self.buf = Buf(name)
        self.chan = None
        self._S = S
        self._name = name
        self._persistent = stack is None

    def ch(self):
        if self.chan is None:
            self.chan = self._S.dma_chan("c_" + self._name, self._persistent)
        return self.chan


class Ring:
    def __init__(self, slots):
        self.slots = slots
        self.i = 0

    def next(self):
        s = self.slots[self.i % len(self.slots)]
        self.i += 1
        return s


class Builder:
    def __init__(self, cfg):
        self.cfg = cfg
        D, DFF, TQ, NM, FH, DH = (cfg[k] for k in ("D", "DFF", "TQ", "NM", "FH", "DH"))
        self.D, self.DFF, self.TQ, self.NM, self.FH, self.DH = D, DFF, TQ, NM, FH, DH
        self.DC = D // 128
        self.FC = DFF // 128
        self.FW = FH * 128
        self.DQ = DH * 256
        self.AC = (self.FW + self.DQ) // 128
        self.TO = TQ + NM
        self.LK = self.TO + TQ
        self.NQB = TQ // 128
        self.NKB = 2 * self.NQB + 1
        self.MKB = self.NQB
        FW, DQ = self.FW, self.DQ
        self.c_fq, self.c_fk, self.c_fv = 0, FW, 2 * FW
        self.c_fl = 3 * FW
        self.c_dq = 3 * FW + FH
        self.c_dk = self.c_dq + DQ
        self.c_dv = self.c_dk + DQ
        self.c_gf = self.c_dv + DQ
        self.c_gd = self.c_gf + D
        self.INC = self.c_gd + D

        nc = bass.Bass("TRN2", target_bir_lowering=False)
        self.nc = nc
        self.S = Sched(nc)

        def din(name, shape, dt=F32):
            return nc.dram_tensor(name, list(shape), dt, kind="ExternalInput").ap()

        def dscr(name, shape, dt):
            return nc.dram_tensor(name, list(shape), dt, kind="Internal").ap()

        TO, LK, DC = self.TO, self.LK, self.DC
        self.xT_own = din("xT_own", [D, TQ])
        self.xT_oth = din("xT_oth", [D, TO])
        self.w = {}
        for nm, shp in (("ff1_w_gate", [D, DFF]), ("ff1_w_up", [D, DFF]), ("ff1_w_down", [DFF, D]),
                        ("w_in", [D, self.INC]), ("w_o_fox", [FW, D]), ("w_o_diff", [DQ, D]),
                        ("w_out", [D, D]), ("ff2_w_gate", [D, DFF]), ("ff2_w_up", [D, DFF]),
                        ("ff2_w_down", [DFF, D])):
            self.w[nm] = din(nm, shp)
        self.gains_d = din("gains", [128, 6 * DC])
        self.bfg_d = din("bfg", [FH, 1])
        self.lamv_d = din("lamv", [128, 4 * 128])
        self.subg_d = din("subg", [128, 2])
        self.tri_d = din("tri", [128, 128])
        self.sel_d = din("sel", [FH, FH * 128])
        self.ident_d = din("ident", [FH, FH])
        self.kpos_d = din("kpos", [128, self.NKB])
        self.exclk_d = din("exclk", [128, self.NKB])
        self.qpos0_d = din("qpos0", [128, self.NQB])
        self.apf_d = din("apf", [FH, 1])
        self.slopes_d = din("slopes", [128, DH])
        self.outT = nc.dram_tensor("outT", [D, TQ], F32, kind="ExternalOutput").ap()
        self.hidT = dscr("hidT", [DFF, TO], BF16)
        self.uTd = dscr("uTd", [D, TO], BF16)
        self.ssqd = dscr("ssqd", [128, TO], F32)
        self.yT = dscr("yT", [D, TO], F32)
        self.h1T = dscr("h1T", [D, TQ], F32)
        self.h2T = dscr("h2T", [D, TQ], F32)
        self.KfT = dscr("KfT", [FW, LK], BF16)
        self.KdT = dscr("KdT", [DQ, LK], BF16)
        self.QfT = dscr("QfT", [FW, TQ], BF16)
        self.QdT = dscr("QdT", [DQ, TQ], BF16)
        self.Vtok = dscr("Vtok", [self.NKB * 128, FW + DQ], BF16)
        self.zT = dscr("zT", [FH, LK], F32)
        self.SG = dscr("SG", [2 * D, TQ], BF16)
        self.attT = dscr("attT", [FW + DQ, TQ], BF16)
        self.mrgT = dscr("mrgT", [D, TQ], BF16)
        self.mixT = dscr("mixT", [D, TQ], F32)
        self.dbufs = {}

        S = self.S
        self.NKW = 43
        self.wring = Ring([Slot(S, nc, f"wr{i}", [128, self.NKW, 128], BF16) for i in range(4)])
        self.psum = []
        for i in range(8):
            t = nc.psum_tensor(f"ps{i}", [128, 512], F32).__enter__()
            self.psum.append((t.ap() if callable(getattr(t, "ap", None)) else t, Buf(f"ps{i}")))
        self.ps_i = 0
        self.cst = {}
        self.cbuf = Buf("consts")
        self.cchan = S.dma_chan("c_consts")

    def dbuf(self, key):
        b = self.dbufs.get(key)
        if b is None:
            b = self.dbufs[key] = Buf(str(key))
        return b

    def bank(self):
        p = self.psum[self.ps_i % 8]
        self.ps_i += 1
        return p

    def tiles(self, T):
        out, t = [], 0
        while t < T:
            n = min(512, T - t)
            out.append((t, n))
            t += n
        return out

    def cload(self, name, src, shape, dt=F32, cast=False):
        nc, S = self.nc, self.S
        t = nc.sbuf_tensor("k_" + name, list(shape), dt).__enter__()
        ap = t.ap() if callable(getattr(t, "ap", None)) else t
        if cast:
            S.dma(S.pool, self.cchan, lambda: nc.gpsimd.dma_start(out=ap, in_=src), writes=[self.cbuf])
        else:
            S.dma(S.sp, self.cchan, lambda: nc.sync.dma_start(out=ap, in_=src), writes=[self.cbuf])
        self.cst[name] = ap
        return ap

    def load_consts(self):
        nc, S = self.nc, self.S
        DC, FH = self.DC, self.FH
        g = self.cload("gains", self.gains_d, [128, 6 * DC])
        self.gain = lambda i: g[:, i * DC:(i + 1) * DC]
        self.cload("bfg", self.bfg_d, [FH, 1])
        self.cload("lamv", self.lamv_d, [128, 4 * 128])
        self.cload("subg", self.subg_d, [128, 2])
        self.cload("tri", self.tri_d, [128, 128], BF16, cast=True)
        self.cload("ident", self.ident_d, [FH, FH])
        self.cload("kpos", self.kpos_d, [128, self.NKB])
        self.cload("exclk", self.exclk_d, [128, self.NKB])
        self.cload("qpos0", self.qpos0_d, [128, self.NQB])
        self.cload("apf", self.apf_d, [FH, 1])
        self.cload("slopes", self.slopes_d, [128, self.DH])
        t = nc.sbuf_tensor("k_ones", [128, 128], BF16).__enter__()
        ones = t.ap() if callable(getattr(t, "ap", None)) else t
        S.op(S.dve, lambda: nc.vector.memset(ones, 1.0), writes=[self.cbuf])
        self.cst["ones"] = ones
        t = nc.sbuf_tensor("k_ones32", [128, 128], F32).__enter__()
        ones32 = t.ap() if callable(getattr(t, "ap", None)) else t
        S.op(S.dve, lambda: nc.vector.memset(ones32, 1.0), writes=[self.cbuf])
        self.cst["ones32"] = ones32
        t = nc.sbuf_tensor("k_lam", [128, 8], F32).__enter__()
        lam = t.ap() if callable(getattr(t, "ap", None)) else t
        t = nc.sbuf_tensor("k_lamtmp", [128, 128], F32).__enter__()
        ltmp = t.ap() if callable(getattr(t, "ap", None)) else t
        lv = self.cst["lamv"]
        cb = self.cbuf
        S.op(S.dve, lambda: nc.vector.tensor_tensor(out=ltmp, in0=lv[:, 0:128], in1=lv[:, 128:256], op=ALU.mult), reads=[cb], writes=[cb])
        S.op(S.dve, lambda: nc.vector.reduce_sum(out=lam[:, 0:1], in_=ltmp, axis=mybir.AxisListType.X), reads=[cb], writes=[cb])
        S.op(S.dve, lambda: nc.vector.tensor_tensor(out=ltmp, in0=lv[:, 256:384], in1=lv[:, 384:512], op=ALU.mult), reads=[cb], writes=[cb])
        S.op(S.dve, lambda: nc.vector.reduce_sum(out=lam[:, 1:2], in_=ltmp, axis=mybir.AxisListType.X), reads=[cb], writes=[cb])
        S.op(S.act, lambda: nc.scalar.activation(out=lam[:, 2:4], in_=lam[:, 0:2], func=AF.Exp), reads=[cb], writes=[cb])
        S.op(S.dve, lambda: nc.vector.tensor_tensor(out=lam[:, 4:5], in0=lam[:, 3:4], in1=lam[:, 2:3], op=ALU.subtract), reads=[cb], writes=[cb])
        S.op(S.dve, lambda: nc.vector.tensor_scalar(out=lam[:, 5:6], in0=lam[:, 4:5], scalar1=-LAMBDA_INIT, scalar2=None, op0=ALU.add), reads=[cb], writes=[cb])
        self.neg_lam = lam[:, 5:6]
        t = nc.sbuf_tensor("k_subg2", [128, 2], F32).__enter__()
        sg2 = t.ap() if callable(getattr(t, "ap", None)) else t
        S.op(S.dve, lambda: nc.vector.tensor_scalar(out=sg2, in0=self.cst["subg"], scalar1=1.0 - LAMBDA_INIT, scalar2=None, op0=ALU.mult), reads=[cb], writes=[cb])
        self.cst["subg2"] = sg2

    def wtile(self, w_ap, k0, nk, c0, ncol):
        nc, S = self.nc, self.S
        assert nk <= self.NKW and ncol <= 128
        s = self.wring.next()
        dst = s.ap[:, 0:nk, 0:ncol]
        src = w_ap[k0 * 128:(k0 + nk) * 128, c0:c0 + ncol].rearrange("(c p) f -> p c f", p=128)
        S.dma(S.pool, s.ch(), lambda: nc.gpsimd.dma_start(out=dst, in_=src), writes=[s.buf])
        return s

    def mm_group(self, ps, psbuf, M, n, lhs_list, rhs_list, reads):
        nc, S = self.nc, self.S
        L = len(lhs_list)
        fns = []
        for i in range(L):
            fns.append(lambda i=i: nc.tensor.matmul(ps[:M, :n], lhsT=lhs_list[i], rhs=rhs_list[i],
                                                    start=(i == 0), stop=(i == L - 1)))
        S.group(S.pe, fns, reads=reads, writes=[psbuf])

    def norm_stage(self, st, T, res_src, y_src, coef, g_post, res_dst, g_pre):
        nc, S = self.nc, self.S
        DC, D = self.DC, self.D
        PC = 4 if DC % 4 == 0 else 1
        NP = DC // PC
        ones = self.cst["ones"]
        cb = self.cbuf
        H = Slot(S, nc, "nH", [128, DC, 512], F32, st) if g_pre is not None else None
        rp = Ring([Slot(S, nc, f"nr{i}", [128, PC, 512], F32, st) for i in range(3)])
        yp = Ring([Slot(S, nc, f"ny{i}", [128, PC, 512], F32, st) for i in range(3)])
        sqr = Ring([Slot(S, nc, f"nsq{i}", [128, PC, 512], BF16, st) for i in range(2)])
        upr = Ring([Slot(S, nc, f"nu{i}", [128, PC, 512], BF16, st) for i in range(2)])
        rsd1 = Slot(S, nc, "nrs1", [128, 512], F32, st)
        ssl = Ring([Slot(S, nc, f"nss{i}", [128, 512], F32, st) for i in range(2)])
        gcs = Slot(S, nc, "ngc", [128, DC], F32, st)
        if y_src is not None:
            S.op(S.dve, lambda: nc.vector.tensor_scalar(out=gcs.ap, in0=g_post, scalar1=float(coef), scalar2=None, op0=ALU.mult), reads=[cb], writes=[gcs.buf])
        rsd2 = Slot(S, nc, "nrs2", [128, 512], F32, st)

        def piece(ap, pc, t0, n):
            return ap[pc * PC * 128:(pc + 1) * PC * 128, t0:t0 + n].rearrange("(c p) t -> p c t", p=128)

        def accum_sq(src_view, src_buf, ps, pb, pc, n):
            sq = sqr.next()
            S.op(S.act, lambda: nc.scalar.activation(out=sq.ap[:, :, 0:n], in_=src_view, func=AF.Square), reads=[src_buf], writes=[sq.buf])
            fns = [lambda c=c: nc.tensor.matmul(ps[:, 0:n], lhsT=ones, rhs=sq.ap[:, c, 0:n], start=(pc == 0 and c == 0), stop=(pc == NP - 1 and c == PC - 1))
                   for c in range(PC)]
            S.group(S.pe, fns, reads=[sq.buf, cb], writes=[pb])

        def finish_rstd(ps, pb, rsd, n):
            S.op(S.act, lambda: nc.scalar.activation(out=rsd.ap[:, 0:n], in_=ps[:, 0:n], func=AF.Sqrt, scale=1.0 / D, bias=self.eps_ap), reads=[pb, cb], writes=[rsd.buf])
            S.op(S.dve, lambda: nc.vector.reciprocal(out=rsd.ap[:, 0:n], in_=rsd.ap[:, 0:n]), reads=[rsd.buf], writes=[rsd.buf])
            return rsd.ap[:, 0:n].unsqueeze(1).to_broadcast([128, PC, n])

        for (t0, n) in self.tiles(T):
            if y_src is not None:
                ps1, pb1 = self.bank()
                sl = ssl.next()
                S.dma(S.sp, sl.ch(), lambda: nc.sync.dma_start(out=sl.ap[:, 0:n], in_=self.ssqd[:, t0:t0 + n]), writes=[sl.buf])
                S.group(S.pe, [lambda: nc.tensor.matmul(ps1[:, 0:n], lhsT=self.cst["ones32"], rhs=sl.ap[:, 0:n], start=True, stop=True)],
                        reads=[sl.buf, cb], writes=[pb1])
                rb1 = finish_rstd(ps1, pb1, rsd1, n)
            if g_pre is not None:
                ps2, pb2 = self.bank()
            for pc in range(NP):
                gsl = slice(pc * PC, (pc + 1) * PC)
                if y_src is not None:
                    y = yp.next()
                    yv = y.ap[:, :, 0:n]
                    S.dma(S.sp, y.ch(), lambda: nc.sync.dma_start(out=yv, in_=piece(y_src, pc, t0, n)), writes=[y.buf])
                    r = rp.next()
                    rv = r.ap[:, :, 0:n]
                    S.dma(S.sp, r.ch(), lambda: nc.sync.dma_start(out=rv, in_=piece(res_src, pc, t0, n)), writes=[r.buf])
                    fns = [lambda c=c: nc.scalar.activation(out=y.ap[:, c, 0:n], in_=y.ap[:, c, 0:n], func=AF.Copy, scale=gcs.ap[:, pc * PC + c:pc * PC + c + 1])
                           for c in range(PC)]
                    S.group(S.act, fns, reads=[y.buf, gcs.buf], writes=[y.buf])
                    S.op(S.dve, lambda: nc.vector.tensor_tensor(out=yv, in0=yv, in1=rb1, op=ALU.mult), reads=[y.buf, rsd1.buf], writes=[y.buf])
                    if H is not None:
                        hv, hbuf = H.ap[:, gsl, 0:n], H.buf
                        S.op(S.dve, lambda: nc.vector.tensor_tensor(out=hv, in0=rv, in1=yv, op=ALU.add), reads=[r.buf, y.buf], writes=[H.buf])
                        dma_src_slot = None
                    else:
                        hv, hbuf = rv, r.buf
                        S.op(S.dve, lambda: nc.vector.tensor_tensor(out=rv, in0=rv, in1=yv, op=ALU.add), reads=[r.buf, y.buf], writes=[r.buf])
                    if res_dst is not None:
                        st_slot = r
                        S.dma(S.pool, st_slot.ch(), lambda: nc.gpsimd.dma_start(out=piece(res_dst, pc, t0, n), in_=hv), reads=[hbuf, r.buf], writes=[r.buf] if H is not None else [])
                else:
                    hv, hbuf = H.ap[:, gsl, 0:n], H.buf
                    S.dma(S.sp, H.ch(), lambda: nc.sync.dma_start(out=hv, in_=piece(res_src, pc, t0, n)), writes=[H.buf])
                if g_pre is not None:
                    accum_sq(hv, hbuf, ps2, pb2, pc, n)
            if g_pre is not None:
                rb2 = finish_rstd(ps2, pb2, rsd2, n)
                for pc in range(NP):
                    gsl = slice(pc * PC, (pc + 1) * PC)
                    tmp = yp.next()
                    tv = tmp.ap[:, :, 0:n]
                    S.op(S.dve, lambda: nc.vector.tensor_tensor(out=tv, in0=H.ap[:, gsl, 0:n], in1=rb2, op=ALU.mult), reads=[H.buf, rsd2.buf], writes=[tmp.buf])
                    u = upr.next()
                    uv = u.ap[:, :, 0:n]
                    fns = [lambda c=c: nc.scalar.activation(out=u.ap[:, c, 0:n], in_=tmp.ap[:, c, 0:n], func=AF.Copy, scale=g_pre[:, pc * PC + c:pc * PC + c + 1])
                           for c in range(PC)]
                    S.group(S.act, fns, reads=[tmp.buf, cb], writes=[u.buf])
                    S.dma(S.pool, u.ch(), lambda: nc.gpsimd.dma_start(out=piece(self.uTd, pc, t0, n), in_=uv), reads=[u.buf])

    def load_tiled(self, st, name, src, nch, T):
        nc, S = self.nc, self.S
        u = Slot(S, nc, name, [128, nch, T], BF16, st)
        bufs = []
        for (t0, n) in self.tiles(T):
            bf = Buf(f"{name}{t0}")
            ch = S.dma_chan(f"c_{name}{t0}", persistent=False)
            S.dma(S.sp, ch, lambda: nc.sync.dma_start(out=u.ap[:, :, t0:t0 + n], in_=src[0:nch * 128, t0:t0 + n].rearrange("(c p) t -> p c t", p=128)), writes=[bf])
            bufs.append(bf)
        return u.ap, bufs

    def load_u(self, st, T):
        nc, S = self.nc, self.S
        DC = self.DC
        u = Slot(S, nc, "uT", [128, DC, T], BF16, st)
        bufs = []
        for (t0, n) in self.tiles(T):
            bf = Buf(f"uT{t0}")
            ch = S.dma_chan(f"c_uT{t0}", persistent=False)
            S.dma(S.sp, ch, lambda: nc.sync.dma_start(out=u.ap[:, :, t0:t0 + n], in_=self.uTd[:, t0:t0 + n].rearrange("(c p) t -> p c t", p=128)), writes=[bf])
            bufs.append(bf)
        return (u.ap, bufs)

    def ffn_gateup(self, st, T, uT, wg, wu):
        nc, S = self.nc, self.S
        DC, FC = self.DC, self.FC
        uT_ap, uT_buf = uT
        tiles = self.tiles(T)
        sgr = Ring([Slot(S, nc, f"fsg{i}", [128, 512], F32, st) for i in range(2)])
        hst = Ring([Slot(S, nc, f"fh{i}", [128, T], BF16, st) for i in range(2)])
        for j in range(FC):
            sg_w = self.wtile(wg, 0, DC, j * 128, 128)
            su_w = self.wtile(wu, 0, DC, j * 128, 128)
            hs = hst.next()
            for (t0, n) in tiles:
                psA, pbA = self.bank()
                self.mm_group(psA, pbA, 128, n, [sg_w.ap[:, c, :] for c in range(DC)],
                              [uT_ap[:, c, t0:t0 + n] for c in range(DC)], [sg_w.buf, uT_buf[t0 // 512]])
                sg = sgr.next()
                S.op(S.act, lambda: nc.scalar.activation(out=sg.ap[:, 0:n], in_=psA[:, 0:n], func=AF.Silu), reads=[pbA], writes=[sg.buf])
                psB, pbB = self.bank()
                self.mm_group(psB, pbB, 128, n, [su_w.ap[:, c, :] for c in range(DC)],
                              [uT_ap[:, c, t0:t0 + n] for c in range(DC)], [su_w.buf, uT_buf[t0 // 512]])
                S.op(S.dve, lambda: nc.vector.tensor_tensor(out=hs.ap[:, t0:t0 + n], in0=sg.ap[:, 0:n], in1=psB[:, 0:n], op=ALU.mult),
                     reads=[sg.buf, pbB], writes=[hs.buf])
            S.dma(S.sp, hs.ch(), lambda: nc.sync.dma_start(out=self.hidT[j * 128:(j + 1) * 128, 0:T], in_=hs.ap[:, 0:T]),
                  reads=[hs.buf], writes=[self.dbuf(("hid", j))])

    def ssq_accum(self, ssq, sqt, src, oc, t0, n):
        nc, S = self.nc, self.S
        if oc == 0:
            S.op(S.act, lambda: nc.scalar.activation(out=ssq.ap[:, t0:t0 + n], in_=src.ap[:, 0:n], func=AF.Square), reads=[src.buf], writes=[ssq.buf])
        else:
            t = sqt.next()
            S.op(S.act, lambda: nc.scalar.activation(out=t.ap[:, 0:n], in_=src.ap[:, 0:n], func=AF.Square), reads=[src.buf], writes=[t.buf])
            S.op(S.dve, lambda: nc.vector.tensor_tensor(out=ssq.ap[:, t0:t0 + n], in0=ssq.ap[:, t0:t0 + n], in1=t.ap[:, 0:n], op=ALU.add),
                 reads=[ssq.buf, t.buf], writes=[ssq.buf])

    def ssq_store(self, ssq, T):
        nc, S = self.nc, self.S
        S.dma(S.sp, ssq.ch(), lambda: nc.sync.dma_start(out=self.ssqd[:, 0:T], in_=ssq.ap[:, 0:T]), reads=[ssq.buf])

    def ffn_down(self, st, T, wd):
        nc, S = self.nc, self.S
        DC, FC = self.DC, self.FC
        tiles = self.tiles(T)
        NQ = 4
        bounds = [(FC * q) // NQ for q in range(NQ + 1)]
        nqmax = max(bounds[q + 1] - bounds[q] for q in range(NQ))
        hq = Slot(S, nc, "dhq", [128, nqmax, T], BF16, st)
        hq_buf = [Buf(f"hq{i}") for i in range(len(tiles))]
        hq_ch = [S.dma_chan(f"c_hq{i}", persistent=False) for i in range(len(tiles))]
        yin = Ring([Slot(S, nc, f"dyi{i}", [128, 512], F32, st) for i in range(3)])
        yout = Ring([Slot(S, nc, f"dyo{i}", [128, 512], F32, st) for i in range(3)])
        ssq = Slot(S, nc, "dssq", [128, T], F32, st)
        sqt = Ring([Slot(S, nc, f"dsq{i}", [128, 512], F32, st) for i in range(2)])
        for q in range(NQ):
            k0, k1 = bounds[q], bounds[q + 1]
            nq = k1 - k0
            for ti, (t0, n) in enumerate(tiles):
                S.dma(S.sp, hq_ch[ti], lambda: nc.sync.dma_start(out=hq.ap[:, 0:nq, t0:t0 + n],
                                                                   in_=self.hidT[k0 * 128:k1 * 128, t0:t0 + n].rearrange("(c p) t -> p c t", p=128)),
                      writes=[hq_buf[ti]])
            for oc in range(DC):
                ws = self.wtile(wd, k0, nq, oc * 128, 128)
                for ti, (t0, n) in enumerate(tiles):
                    ps, pb = self.bank()
                    self.mm_group(ps, pb, 128, n, [ws.ap[:, c, :] for c in range(nq)],
                                  [hq.ap[:, c, t0:t0 + n] for c in range(nq)], [ws.buf, hq_buf[ti]])
                    yb = self.dbuf(("y", oc, ti))
                    yo = yout.next()
                    if q == 0:
                        S.op(S.act, lambda: nc.scalar.copy(out=yo.ap[:, 0:n], in_=ps[:, 0:n]), reads=[pb], writes=[yo.buf])
                    else:
                        yi = yin.next()
                        S.dma(S.sp, yi.ch(), lambda: nc.sync.dma_start(out=yi.ap[:, 0:n], in_=self.yT[oc * 128:(oc + 1) * 128, t0:t0 + n]),
                              reads=[yb], writes=[yi.buf])
                        S.op(S.dve, lambda: nc.vector.tensor_tensor(out=yo.ap[:, 0:n], in0=ps[:, 0:n], in1=yi.ap[:, 0:n], op=ALU.add),
                             reads=[pb, yi.buf], writes=[yo.buf])
                    S.dma(S.act, yo.ch(), lambda: nc.scalar.dma_start(out=self.yT[oc * 128:(oc + 1) * 128, t0:t0 + n], in_=yo.ap[:, 0:n]),
                          reads=[yo.buf], writes=[yb])
                    if q == NQ - 1:
                        self.ssq_accum(ssq, sqt, yo, oc, t0, n)
        self.ssq_store(ssq, T)

    def proj(self, st, T, uT, own):
        nc, S = self.nc, self.S
        DC, FH, DH, FW, DQ = self.DC, self.FH, self.DH, self.FW, self.DQ
        uT_ap, uT_buf = uT
        win = self.w["w_in"]
        tiles = self.tiles(T)
        koff = self.TO if own else 0
        ost = Ring([Slot(S, nc, f"po{i}", [128, T], BF16, st) for i in range(2)])

        def fm(c0, nchunks, dst, dst_key, tcol0, func=None):
            for j in range(nchunks):
                ws = self.wtile(win, 0, DC, c0 + j * 128, 128)
                o = ost.next()
                for (t0, n) in tiles:
                    ps, pb = self.bank()
                    self.mm_group(ps, pb, 128, n, [ws.ap[:, c, :] for c in range(DC)],
                                  [uT_ap[:, c, t0:t0 + n] for c in range(DC)], [ws.buf, uT_buf[t0 // 512]])
                    if func is None:
                        eng = S.act if (self.ps_i % 2 == 0) else S.dve
                        if eng is S.act:
                            S.op(S.act, lambda: nc.scalar.copy(out=o.ap[:, t0:t0 + n], in_=ps[:, 0:n]), reads=[pb], writes=[o.buf])
                        else:
                            S.op(S.dve, lambda: nc.vector.tensor_copy(out=o.ap[:, t0:t0 + n], in_=ps[:, 0:n]), reads=[pb], writes=[o.buf])
                    else:
                        S.op(S.act, lambda: nc.scalar.activation(out=o.ap[:, t0:t0 + n], in_=ps[:, 0:n], func=func), reads=[pb], writes=[o.buf])
                S.dma(S.sp, o.ch(), lambda: nc.sync.dma_start(out=dst[j * 128:(j + 1) * 128, tcol0:tcol0 + T], in_=o.ap[:, 0:T]),
                      reads=[o.buf], writes=[self.dbuf((dst_key, j))])

        fm(self.c_fk, FW // 128, self.KfT, "KfT" + str(own), koff)
        fm(self.c_dk, DQ // 128, self.KdT, "KdT" + str(own), koff)
        if own:
            fm(self.c_fq, FW // 128, self.QfT, "QfT", 0)
            fm(self.c_dq, DQ // 128, self.QdT, "QdT", 0)
            fm(self.c_gf, 2 * self.D // 128, self.SG, "SG", 0, func=AF.Sigmoid)
        ws = self.wtile(win, 0, DC, self.c_fl, FH)
        zs = Slot(S, nc, "pz", [FH, T], F32, st)
        for (t0, n) in tiles:
            ps, pb = self.bank()
            self.mm_group(ps, pb, FH, n, [ws.ap[:, c, 0:FH] for c in range(DC)],
                          [uT_ap[:, c, t0:t0 + n] for c in range(DC)], [ws.buf, uT_buf[t0 // 512]])
            S.op(S.act, lambda: nc.scalar.copy(out=zs.ap[:, t0:t0 + n], in_=ps[:FH, 0:n]), reads=[pb], writes=[zs.buf])
        S.dma(S.sp, zs.ch(), lambda: nc.sync.dma_start(out=self.zT[:, koff:koff + T], in_=zs.ap[:, 0:T]), reads=[zs.buf],
              writes=[self.dbuf(("zT", own))])
        nvc = (FW + DQ) // 128
        tb = [(t, min(128, T - t)) for t in range(0, T, 128)]
        vst = Ring([Slot(S, nc, f"pv{i}", [128, 4, 128], BF16, st) for i in range(3)])
        kb0 = (self.MKB + 1) if own else 0
        for j in range(nvc):
            c0 = (self.c_fv + j * 128) if j < FW // 128 else (self.c_dv + (j - FW // 128) * 128)
            ws = self.wtile(win, 0, DC, c0, 128)
            for g0 in range(0, len(tb), 4):
                grp = tb[g0:g0 + 4]
                ps, pb = self.bank()
                vs = vst.next()
                fns = []
                for gi, (t0, n) in enumerate(grp):
                    for c in range(DC):
                        fns.append(lambda gi=gi, t0=t0, n=n, c=c: nc.tensor.matmul(
                            ps[:n, gi * 128:(gi + 1) * 128], lhsT=uT_ap[:, c, t0:t0 + n], rhs=ws.ap[:, c, :],
                            start=(c == 0), stop=(c == DC - 1)))
                S.group(S.pe, fns, reads=[ws.buf, uT_buf[grp[0][0] // 512]], writes=[pb])
                nfull = sum(1 for (_, n) in grp if n == 128)
                if nfull:
                    S.op(S.dve, lambda: nc.vector.tensor_copy(out=vs.ap[:, 0:nfull, :], in_=ps[:, 0:nfull * 128].rearrange("p (g f) -> p g f", f=128)),
                         reads=[pb], writes=[vs.buf])
                    kb = kb0 + g0
                    S.dma(S.sp, vs.ch(), lambda: nc.sync.dma_start(
                        out=self.Vtok[kb * 128:(kb + nfull) * 128, j * 128:(j + 1) * 128].rearrange("(g p) f -> p g f", p=128),
                        in_=vs.ap[:, 0:nfull, :]), reads=[vs.buf], writes=[self.dbuf(("V", own, j))])
                if nfull < len(grp):
                    gi = nfull
                    n = grp[gi][1]
                    vs2 = vst.next()
                    S.op(S.dve, lambda: nc.vector.tensor_copy(out=vs2.ap[:n, 0, :], in_=ps[:n, gi * 128:(gi + 1) * 128]), reads=[pb], writes=[vs2.buf])
                    kb = kb0 + g0 + gi
                    S.dma(S.sp, vs2.ch(), lambda: nc.sync.dma_start(out=self.Vtok[kb * 128:kb * 128 + n, j * 128:(j + 1) * 128], in_=vs2.ap[:n, 0, :]),
                          reads=[vs2.buf], writes=[self.dbuf(("V", own, j))])

    def attn_prep(self, st_out, st):
        nc, S = self.nc, self.S
        FH, TQ, TO, NM, LK, NKB, NQB, MKB = self.FH, self.TQ, self.TO, self.NM, self.LK, self.NKB, self.NQB, self.MKB
        cb = self.cbuf
        refB = Slot(S, nc, "arefB", [128, FH, NQB], F32, st_out)
        biask = Slot(S, nc, "abk", [128, NKB, FH], F32, st_out)
        dpos = Slot(S, nc, "adpos", [128, NKB, NQB], F32, st_out)
        z = Slot(S, nc, "az", [FH, LK], F32, st)
        S.dma(S.sp, z.ch(), lambda: nc.sync.dma_start(out=z.ap, in_=self.zT), reads=[self.dbuf(("zT", False)), self.dbuf(("zT", True))], writes=[z.buf])
        a = Slot(S, nc, "aa", [FH, LK], F32, st)
        m = Slot(S, nc, "am", [FH, LK], F32, st)
        zb = [z.buf, a.buf, m.buf]
        S.op(S.dve, lambda: nc.vector.tensor_scalar(out=z.ap, in0=z.ap, scalar1=self.cst["bfg"][:, 0:1], scalar2=None, op0=ALU.add), reads=[z.buf, cb], writes=[z.buf])
        S.op(S.act, lambda: nc.scalar.activation(out=a.ap, in_=z.ap, func=AF.Abs), reads=[z.buf], writes=[a.buf])
        S.op(S.act, lambda: nc.scalar.activation(out=a.ap, in_=a.ap, func=AF.Exp, scale=-1.0), reads=[a.buf], writes=[a.buf])
        S.op(S.act, lambda: nc.scalar.activation(out=a.ap, in_=a.ap, func=AF.Ln, bias=self.one_ap[:FH, :], scale=1.0), reads=[a.buf, cb], writes=[a.buf])
        S.op(S.dve, lambda: nc.vector.tensor_scalar(out=m.ap, in0=z.ap, scalar1=0.0, scalar2=None, op0=ALU.min), reads=[z.buf], writes=[m.buf])
        S.op(S.dve, lambda: nc.vector.tensor_tensor(out=m.ap, in0=m.ap, in1=a.ap, op=ALU.subtract), reads=[m.buf, a.buf], writes=[m.buf])
        onesf = Slot(S, nc, "aones", [FH, TQ], F32, st)
        S.op(S.dve, lambda: nc.vector.memset(onesf.ap, 1.0), writes=[onesf.buf])
        for (s0, s1) in ((0, TQ), (TQ, TO), (TO, LK)):
            S.op(S.dve, lambda s0=s0, s1=s1: nc.vector.tensor_tensor_scan(out=z.ap[:, s0:s1], data0=onesf.ap[:, 0:s1 - s0], data1=m.ap[:, s0:s1],
                                                                           initial=0.0, op0=ALU.mult, op1=ALU.add),
                 reads=[m.buf, onesf.buf], writes=[z.buf])
        off = Slot(S, nc, "aoff", [FH, 2], F32, st)
        S.op(S.dve, lambda: nc.vector.tensor_copy(out=off.ap[:, 0:1], in_=z.ap[:, TO - 1:TO]), reads=[z.buf], writes=[off.buf])
        S.op(S.dve, lambda: nc.vector.scalar_tensor_tensor(out=off.ap[:, 1:2], in0=z.ap[:, TQ - 1:TQ], scalar=self.cst["apf"][:, 0:1], in1=off.ap[:, 0:1],
                                                           op0=ALU.mult, op1=ALU.add), reads=[z.buf, off.buf, cb], writes=[off.buf])
        S.op(S.dve, lambda: nc.vector.tensor_scalar(out=z.ap[:, 0:TQ], in0=z.ap[:, 0:TQ], scalar1=off.ap[:, 0:1], scalar2=None, op0=ALU.add), reads=[z.buf, off.buf], writes=[z.buf])
        S.op(S.dve, lambda: nc.vector.tensor_scalar(out=z.ap[:, TO:LK], in0=z.ap[:, TO:LK], scalar1=off.ap[:, 1:2], scalar2=None, op0=ALU.add), reads=[z.buf, off.buf], writes=[z.buf])
        ctok = Slot(S, nc, "actok", [128, NKB, FH], F32, st)
        S.op(S.dve, lambda: nc.vector.memset(ctok.ap, 0.0), writes=[ctok.buf])
        ident = self.cst["ident"]
        for kb in range(NKB):
            if kb < MKB:
                c0, n = kb * 128, 128
            elif kb == MKB:
                c0, n = TQ, NM
            else:
                c0, n = TO + (kb - MKB - 1) * 128, 128
            ps, pb = self.bank()
            S.group(S.pe, [lambda: nc.tensor.transpose(out=ps[:n, 0:FH], in_=z.ap[:, c0:c0 + n], identity=ident)], reads=[z.buf, cb], writes=[pb])
            S.op(S.dve, lambda: nc.vector.tensor_copy(out=ctok.ap[:n, kb, :], in_=ps[:n, 0:FH]), reads=[pb], writes=[ctok.buf])
        ps, pb = self.bank()
        zown_first = z.ap[:, TO:LK].rearrange("h (q i) -> h q i", i=128)[:, :, 0]
        cf = Slot(S, nc, "acf", [FH, NQB], F32, st)
        S.op(S.dve, lambda: nc.vector.tensor_copy(out=cf.ap, in_=zown_first), reads=[z.buf], writes=[cf.buf])
        selS = Slot(S, nc, "asel", [FH, FH * 128], F32, st)
        S.dma(S.sp, selS.ch(), lambda: nc.sync.dma_start(out=selS.ap, in_=self.sel_d), writes=[selS.buf])
        sel = selS.ap
        fns = [lambda h=h: nc.tensor.matmul(ps[:, h * NQB:(h + 1) * NQB], lhsT=sel[:, h * 128:(h + 1) * 128], rhs=cf.ap, start=True, stop=True)
               for h in range(FH)]
        S.group(S.pe, fns, reads=[cf.buf, selS.buf], writes=[pb])
        S.op(S.dve, lambda: nc.vector.tensor_copy(out=refB.ap, in_=ps[:, 0:FH * NQB].rearrange("p (h q) -> p h q", q=NQB)), reads=[pb], writes=[refB.buf])
        S.op(S.dve, lambda: nc.vector.tensor_tensor(out=biask.ap, in0=self.cst["exclk"].unsqueeze(2).to_broadcast([128, NKB, FH]), in1=ctok.ap, op=ALU.subtract),
             reads=[ctok.buf, cb], writes=[biask.buf])
        S.op(S.dve, lambda: nc.vector.tensor_tensor(out=dpos.ap, in0=self.cst["kpos"].unsqueeze(2).to_broadcast([128, NKB, NQB]),
                                                    in1=self.cst["qpos0"].unsqueeze(1).to_broadcast([128, NKB, NQB]), op=ALU.subtract),
             reads=[cb], writes=[dpos.buf])
        return refB, biask, dpos

    def attention(self, st):
        nc, S = self.nc, self.S
        FH, DH, TQ, TO, NM, LK, NKB, NQB, MKB, FW = self.FH, self.DH, self.TQ, self.TO, self.NM, self.LK, self.NKB, self.NQB, self.MKB, self.FW
        cb = self.cbuf
        with contextlib.ExitStack() as st_tmp:
            refB, biask, dpos = self.attn_prep(st, st_tmp)
            S.barrier()
        scale = 1.0 / math.sqrt(128.0)
        ones = self.cst["ones"]
        tri = self.cst["tri"]
        NQT = TQ // 512
        kts = Ring([Slot(S, nc, f"tk{i}", [128, LK], BF16, st) for i in range(4)])
        qts = Ring([Slot(S, nc, f"tq{i}", [128, TQ], BF16, st) for i in range(4)])
        vts = Ring([Slot(S, nc, f"tv{i}", [128, NKB, 256], BF16, st) for i in range(2)])
        bms = Ring([Slot(S, nc, f"tb{i}", [128, NKB, NQB], F32, st) for i in range(2)])
        pts = Ring([Slot(S, nc, f"tp{i}", [128, 512], BF16, st) for i in range(8)])
        rinv = Slot(S, nc, "trinv", [128, 512], F32, st)
        racc = Ring([Slot(S, nc, f"tra{i}", [128, 512], F32, st) for i in range(2)])
        evs = Ring([Slot(S, nc, f"tev{i}", [128, 3, 512], F32, st) for i in range(2)])
        ost = Ring([Slot(S, nc, f"to{i}", [128, 512], BF16, st) for i in range(2)])
        t1 = Slot(S, nc, "tt1", [128, 2, 512], F32, st)
        t2 = Slot(S, nc, "tt2", [128, 2, 512], F32, st)
        sq = Slot(S, nc, "tsq", [128, 2, 512], BF16, st)
        rsd = Slot(S, nc, "trsd", [128, 512], F32, st)

        def load_v(vt, col0, ncol):
            segs = [(0, MKB, 0), (MKB + 1, NKB, MKB + 1)]
            for (b0, b1, _) in segs:
                S.dma(S.sp, vt.ch(), lambda b0=b0, b1=b1: nc.sync.dma_start(
                    out=vt.ap[:, b0:b1, 0:ncol], in_=self.Vtok[b0 * 128:b1 * 128, col0:col0 + ncol].rearrange("(g p) f -> p g f", p=128)),
                    reads=[self.dbuf(("V", o, j)) for o in (False, True) for j in range((col0) // 128, (col0 + ncol) // 128)], writes=[vt.buf])
            S.dma(S.sp, vt.ch(), lambda: nc.sync.dma_start(out=vt.ap[:NM, MKB, 0:ncol], in_=self.Vtok[MKB * 128:MKB * 128 + NM, col0:col0 + ncol]),
                  reads=[self.dbuf(("V", False, j)) for j in range((col0) // 128, (col0 + ncol) // 128)], writes=[vt.buf])

        def kblocks(qt):
            lst = [(kb, 128, 0) for kb in range(MKB)]
            lst.append((MKB, NM, 0))
            for kbo in range(4 * qt + 4):
                r = kbo - 4 * qt
                lst.append((MKB + 1 + kbo, 128, 128 * r if r > 0 else 0))
            return lst

        sbanks = [self.psum[i] for i in (0, 1, 2, 3, 4)]
        abanks = [[self.psum[i] for i in (5, 6, 7)]]
        cnt = {"s": 0, "a": 0}
        LOOK = 4

        def sbank():
            cnt["s"] += 1
            return sbanks[cnt["s"] % len(sbanks)]

        def aset():
            cnt["a"] += 1
            return abanks[cnt["a"] % len(abanks)]

        def head_pass(kt, qs, vt, ndv, bm, qt, ps_o, ps_r, wsub=128):
            kl = kblocks(qt)
            nb = len(kl)

            def qk_fn(idx):
                kb, rows, c0 = kl[idx]
                kc0 = TQ if kb == MKB else (kb * 128 if kb < MKB else TO + (kb - MKB - 1) * 128)
                ps, pb = sbank()
                fn = lambda: nc.tensor.matmul(ps[:rows, :512], lhsT=kt.ap[:, kc0:kc0 + rows], rhs=qs.ap[:, qt * 512:(qt + 1) * 512], start=True, stop=True)
                return fn, (ps, pb)

            def qk(idx):
                fn, (ps, pb) = qk_fn(idx)
                S.group(S.pe, [fn], reads=[kt.buf, qs.buf], writes=[pb])
                return ps, pb

            ra = racc.next()
            S.op(S.dve, lambda: nc.vector.memset(ra.ap, 0.0), writes=[ra.buf])
            pend = [qk(i) for i in range(min(LOOK, nb))]
            for idx, (kb, rows, c0) in enumerate(kl):
                first, last = idx == 0, idx == nb - 1
                qk_extra = None
                if idx + LOOK < nb:
                    qk_extra = qk_fn(idx + LOOK)
                    pend.append(qk_extra[1])
                ps, pb = pend.pop(0)
                p = pts.next()
                kbo = kb - MKB - 1
                segs = []
                for a0 in range(0, 512, wsub):
                    lo, hi = max(a0, c0), a0 + wsub
                    if lo < hi:
                        segs.append((lo, hi, qt * 4 + a0 // 128))
                fns = [lambda lo=lo, hi=hi, bc=bc: nc.scalar.activation(out=p.ap[:rows, lo:hi], in_=ps[:rows, lo:hi], func=AF.Exp,
                                                                         bias=bm.ap[:rows, kb, bc:bc + 1], scale=scale)
                       for (lo, hi, bc) in segs]
                S.group(S.act, fns, reads=[pb, bm.buf], writes=[p.buf])
                if kb > MKB and kbo >= 4 * qt:
                    j = kbo - 4 * qt
                    S.op(S.pool, lambda j=j: nc.gpsimd.tensor_tensor(out=p.ap[:, j * 128:(j + 1) * 128], in0=p.ap[:, j * 128:(j + 1) * 128], in1=tri, op=ALU.mult),
                         reads=[p.buf, cb], writes=[p.buf])
                fns = []
                for i in range(ndv):
                    fns.append(lambda i=i: nc.tensor.matmul(ps_o[i][0][:, c0:512], lhsT=vt.ap[:rows, kb, i * 128:(i + 1) * 128], rhs=p.ap[:rows, c0:512],
                                                            start=first, stop=last))
                if qk_extra is not None:
                    S.group(S.pe, [qk_extra[0]] + fns, reads=[p.buf, vt.buf, cb, kt.buf, qs.buf], writes=[ps_o[i][1] for i in range(ndv)] + [qk_extra[1][1]])
                else:
                    S.group(S.pe, fns, reads=[p.buf, vt.buf, cb], writes=[ps_o[i][1] for i in range(ndv)])
                S.op(S.dve, lambda: nc.vector.tensor_tensor(out=ra.ap[:rows, c0:512], in0=ra.ap[:rows, c0:512], in1=p.ap[:rows, c0:512], op=ALU.add),
                     reads=[ra.buf, p.buf], writes=[ra.buf])
            S.group(S.pe, [lambda: nc.tensor.matmul(ps_r[0], lhsT=self.cst["ones32"], rhs=ra.ap, start=True, stop=True)], reads=[ra.buf, cb], writes=[ps_r[1]])

        sub2 = self.cst["subg2"]

        def fox_load(h):
            kt, qs, vt, bm = kts.next(), qts.next(), vts.next(), bms.next()
            S.dma(S.sp, kt.ch(), lambda: nc.sync.dma_start(out=kt.ap, in_=self.KfT[h * 128:(h + 1) * 128, :]), writes=[kt.buf])
            S.dma(S.sp, qs.ch(), lambda: nc.sync.dma_start(out=qs.ap, in_=self.QfT[h * 128:(h + 1) * 128, :]), writes=[qs.buf])
            load_v(vt, h * 128, 128)
            S.op(S.dve, lambda: nc.vector.tensor_tensor(out=bm.ap, in0=biask.ap[:, :, h].unsqueeze(2).to_broadcast([128, NKB, NQB]),
                                                        in1=refB.ap[:, h, :].unsqueeze(1).to_broadcast([128, NKB, NQB]), op=ALU.add),
                 reads=[biask.buf, refB.buf], writes=[bm.buf])
            return (kt, qs, vt, bm)

        def fox_compute(h, res):
            kt, qs, vt, bm = res
            for qt in range(NQT):
                bs = aset()
                ps_o = [bs[0]]
                ps_r = bs[2]
                head_pass(kt, qs, vt, 1, bm, qt, ps_o, ps_r)
                ev = evs.next()
                S.op(S.dve, lambda: nc.vector.tensor_copy(out=ev.ap[:, 0, :], in_=ps_o[0][0]), reads=[ps_o[0][1]], writes=[ev.buf])
                S.op(S.dve, lambda: nc.vector.reciprocal(out=rinv.ap, in_=ps_r[0]), reads=[ps_r[1]], writes=[rinv.buf])
                o = ost.next()
                S.op(S.dve, lambda: nc.vector.tensor_tensor(out=o.ap, in0=ev.ap[:, 0, :], in1=rinv.ap, op=ALU.mult), reads=[ev.buf, rinv.buf], writes=[o.buf])
                S.dma(S.pool, o.ch(), lambda: nc.gpsimd.dma_start(out=self.attT[h * 128:(h + 1) * 128, qt * 512:(qt + 1) * 512], in_=o.ap), reads=[o.buf])

        def diff_load(h):
            vt = vts.next()
            load_v(vt, FW + h * 256, 256)
            bm = bms.next()
            slope = 2.0 ** (-8.0 * (h + 1) / DH)
            S.op(S.dve, lambda: nc.vector.tensor_scalar(out=bm.ap, in0=dpos.ap, scalar1=float(slope), scalar2=None, op0=ALU.mult), reads=[dpos.buf], writes=[bm.buf])
            comps = []
            for c in range(2):
                kt, qs = kts.next(), qts.next()
                r0 = (h * 2 + c) * 128
                S.dma(S.sp, kt.ch(), lambda: nc.sync.dma_start(out=kt.ap, in_=self.KdT[r0:r0 + 128, :]), writes=[kt.buf])
                S.dma(S.sp, qs.ch(), lambda: nc.sync.dma_start(out=qs.ap, in_=self.QdT[r0:r0 + 128, :]), writes=[qs.buf])
                comps.append((kt, qs))
            return (vt, bm, comps)

        def diff_compute(h, res):
            vt, bm, comps = res
            slope = 2.0 ** (-8.0 * (h + 1) / DH)
            wsub = 512 if slope * 511 <= 64.0 else (256 if slope * 255 <= 64.0 else 128)
            for qt in range(NQT):
                for c in range(2):
                    kt, qs = comps[c]
                    bs = aset()
                    ps_o = [bs[0], bs[1]]
                    ps_r = bs[2]
                    head_pass(kt, qs, vt, 2, bm, qt, ps_o, ps_r, wsub=wsub)
                    ev = evs.next()
                    for i in range(2):
                        S.op(S.dve, lambda i=i: nc.vector.tensor_copy(out=ev.ap[:, i, :], in_=ps_o[i][0]), reads=[ps_o[i][1]], writes=[ev.buf])
                    S.op(S.dve, lambda: nc.vector.reciprocal(out=rinv.ap, in_=ps_r[0]), reads=[ps_r[1]], writes=[rinv.buf])
                    tt = t1 if c == 0 else t2
                    for i in range(2):
                        S.op(S.dve, lambda i=i: nc.vector.tensor_tensor(out=tt.ap[:, i, :], in0=ev.ap[:, i, :], in1=rinv.ap, op=ALU.mult),
                             reads=[ev.buf, rinv.buf], writes=[tt.buf])
                S.op(S.dve, lambda: nc.vector.scalar_tensor_tensor(out=t1.ap, in0=t2.ap, scalar=self.neg_lam, in1=t1.ap, op0=ALU.mult, op1=ALU.add),
                     reads=[t1.buf, t2.buf, cb], writes=[t1.buf])
                S.op(S.act, lambda: nc.scalar.activation(out=sq.ap, in_=t1.ap, func=AF.Square), reads=[t1.buf], writes=[sq.buf])
                ps, pb = sbank()
                self.mm_group(ps, pb, 128, 512, [ones, ones], [sq.ap[:, 0, :], sq.ap[:, 1, :]], [sq.buf, cb])
                S.op(S.act, lambda: nc.scalar.activation(out=rsd.ap, in_=ps, func=AF.Sqrt, scale=1.0 / 256.0, bias=self.eps_ap), reads=[pb, cb], writes=[rsd.buf])
                S.op(S.dve, lambda: nc.vector.reciprocal(out=rsd.ap, in_=rsd.ap), reads=[rsd.buf], writes=[rsd.buf])
                for i in range(2):
                    o = ost.next()
                    S.op(S.dve, lambda i=i: nc.vector.scalar_tensor_tensor(out=o.ap, in0=t1.ap[:, i, :], scalar=sub2[:, i:i + 1], in1=rsd.ap, op0=ALU.mult, op1=ALU.mult),
                         reads=[t1.buf, rsd.buf, cb], writes=[o.buf])
                    r0 = FW + h * 256 + i * 128
                    S.dma(S.pool, o.ch(), lambda: nc.gpsimd.dma_start(out=self.attT[r0:r0 + 128, qt * 512:(qt + 1) * 512], in_=o.ap), reads=[o.buf])

        jobs = [(fox_load, fox_compute, h) for h in range(FH)] + [(diff_load, diff_compute, h) for h in range(DH)]
        res = jobs[0][0](jobs[0][2])
        for ji, (ld, cp, h) in enumerate(jobs):
            nres = jobs[ji + 1][0](jobs[ji + 1][2]) if ji + 1 < len(jobs) else None
            cp(h, res)
            res = nres

    def merge(self, st):
        nc, S = self.nc, self.S
        DC, AC, TQ, FW, D = self.DC, self.AC, self.TQ, self.FW, self.D
        att_ap, att_bufs = self.load_tiled(st, "matt", self.attT, AC, TQ)
        nf = FW // 128
        nd = AC - nf
        gst = Ring([Slot(S, nc, f"mg{i}", [128, 2, 512], BF16, st) for i in range(3)])
        m1 = Ring([Slot(S, nc, f"mm{i}", [128, 512], F32, st) for i in range(2)])
        m2 = Ring([Slot(S, nc, f"mn{i}", [128, 512], F32, st) for i in range(2)])
        ost = Ring([Slot(S, nc, f"mo{i}", [128, 512], BF16, st) for i in range(3)])
        for oc in range(DC):
            wf = self.wtile(self.w["w_o_fox"], 0, nf, oc * 128, 128)
            wd = self.wtile(self.w["w_o_diff"], 0, nd, oc * 128, 128)
            for (t0, n) in self.tiles(TQ):
                g = gst.next()
                for i in range(2):
                    S.dma(S.sp, g.ch(), lambda i=i: nc.sync.dma_start(out=g.ap[:, i, 0:n], in_=self.SG[i * D + oc * 128:i * D + (oc + 1) * 128, t0:t0 + n]),
                          writes=[g.buf])
                psA, pbA = self.bank()
                self.mm_group(psA, pbA, 128, n, [wf.ap[:, c, :] for c in range(nf)], [att_ap[:, c, t0:t0 + n] for c in range(nf)], [wf.buf, att_bufs[t0 // 512]])
                psB, pbB = self.bank()
                self.mm_group(psB, pbB, 128, n, [wd.ap[:, c, :] for c in range(nd)], [att_ap[:, nf + c, t0:t0 + n] for c in range(nd)], [wd.buf, att_bufs[t0 // 512]])
                a, b = m1.next(), m2.next()
                o = ost.next()
                S.op(S.dve, lambda: nc.vector.tensor_tensor(out=a.ap[:, 0:n], in0=psA[:, 0:n], in1=g.ap[:, 0, 0:n], op=ALU.mult), reads=[pbA, g.buf], writes=[a.buf])
                S.op(S.dve, lambda: nc.vector.tensor_tensor(out=b.ap[:, 0:n], in0=psB[:, 0:n], in1=g.ap[:, 1, 0:n], op=ALU.mult), reads=[pbB, g.buf], writes=[b.buf])
                S.op(S.dve, lambda: nc.vector.tensor_tensor(out=o.ap[:, 0:n], in0=a.ap[:, 0:n], in1=b.ap[:, 0:n], op=ALU.add), reads=[a.buf, b.buf], writes=[o.buf])
                S.dma(S.sp, o.ch(), lambda: nc.sync.dma_start(out=self.mrgT[oc * 128:(oc + 1) * 128, t0:t0 + n], in_=o.ap[:, 0:n]), reads=[o.buf])

    def outproj(self, st):
        nc, S = self.nc, self.S
        DC, TQ = self.DC, self.TQ
        mg_ap, mg_bufs = self.load_tiled(st, "omg", self.mrgT, DC, TQ)
        ost = Ring([Slot(S, nc, f"oo{i}", [128, 512], F32, st) for i in range(3)])
        ssq = Slot(S, nc, "ossq", [128, TQ], F32, st)
        sqt = Ring([Slot(S, nc, f"osq{i}", [128, 512], F32, st) for i in range(2)])
        for oc in range(DC):
            ws = self.wtile(self.w["w_out"], 0, DC, oc * 128, 128)
            for ti, (t0, n) in enumerate(self.tiles(TQ)):
                ps, pb = self.bank()
                self.mm_group(ps, pb, 128, n, [ws.ap[:, c, :] for c in range(DC)], [mg_ap[:, c, t0:t0 + n] for c in range(DC)], [ws.buf, mg_bufs[t0 // 512]])
                o = ost.next()
                if (oc + ti) % 2 == 0:
                    S.op(S.act, lambda: nc.scalar.copy(out=o.ap[:, 0:n], in_=ps[:, 0:n]), reads=[pb], writes=[o.buf])
                else:
                    S.op(S.dve, lambda: nc.vector.tensor_copy(out=o.ap[:, 0:n], in_=ps[:, 0:n]), reads=[pb], writes=[o.buf])
                S.dma(S.sp, o.ch(), lambda: nc.sync.dma_start(out=self.mixT[oc * 128:(oc + 1) * 128, t0:t0 + n], in_=o.ap[:, 0:n]), reads=[o.buf],
                      writes=[self.dbuf(("mix", oc, ti))])
                self.ssq_accum(ssq, sqt, o, oc, t0, n)
        self.ssq_store(ssq, TQ)

    def build(self):
        nc, S = self.nc, self.S
        DC, TQ, TO = self.DC, self.TQ, self.TO
        self.load_consts()
        t = nc.sbuf_tensor("k_eps", [128, 2], F32).__enter__()
        e = t.ap() if callable(getattr(t, "ap", None)) else t
        S.op(S.dve, lambda: nc.vector.memset(e[:, 0:1], RMS_EPS), writes=[self.cbuf])
        S.op(S.dve, lambda: nc.vector.memset(e[:, 1:2], 1.0), writes=[self.cbuf])
        self.eps_ap = e[:, 0:1]
        self.one_ap = e[:, 1:2]
        G = self.gain
        for own in (False, True):
            T = TQ if own else TO
            xsrc = self.xT_own if own else self.xT_oth
            with contextlib.ExitStack() as st:
                self.norm_stage(st, T, xsrc, None, 0, None, None, G(0))
                S.barrier()
            with contextlib.ExitStack() as st:
                uT = self.load_u(st, T)
                self.ffn_gateup(st, T, uT, self.w["ff1_w_gate"], self.w["ff1_w_up"])
                S.barrier()
            with contextlib.ExitStack() as st:
                self.ffn_down(st, T, self.w["ff1_w_down"])
                S.barrier()
            with contextlib.ExitStack() as st:
                self.norm_stage(st, T, xsrc, self.yT, 0.5, G(1), self.h1T if own else None, G(2))
                S.barrier()
            with contextlib.ExitStack() as st:
                uT = self.load_u(st, T)
                self.proj(st, T, uT, own)
                S.barrier()
        with contextlib.ExitStack() as st:
            self.attention(st)
            S.barrier()
        with contextlib.ExitStack() as st:
            self.merge(st)
            S.barrier()
        with contextlib.ExitStack() as st:
            self.outproj(st)
            S.barrier()
        with contextlib.ExitStack() as st:
            self.norm_stage(st, TQ, self.h1T, self.mixT, 1.0, G(3), self.h2T, G(4))
            S.barrier()
        with contextlib.ExitStack() as st:
            uT = self.load_u(st, TQ)
            self.ffn_gateup(st, TQ, uT, self.w["ff2_w_gate"], self.w["ff2_w_up"])
            S.barrier()
        with contextlib.ExitStack() as st:
            self.ffn_down(st, TQ, self.w["ff2_w_down"])
            S.barrier()
        with contextlib.ExitStack() as st:
            self.norm_stage(st, TQ, self.h2T, self.yT, 0.5, G(5), self.outT, None)
            S.barrier()
        return nc


def host_inputs(cfg, inp):
    D, DFF, TQ, NM, FH, DH, B = (cfg[k] for k in ("D", "DFF", "TQ", "NM", "FH", "DH", "B"))
    DC = D // 128
    NQB = TQ // 128
    NKB = 2 * NQB + 1
    f32 = np.float32
    x = np.asarray(inp["x"], f32)
    meta = np.asarray(inp["meta_tokens"], f32)

    def pc(v):
        return np.ascontiguousarray(np.asarray(v, f32).reshape(-1, 128).T)

    gains = np.concatenate([pc(inp[k][0]) for k in ("ff1_pre_g", "ff1_post_g", "mix_pre_g", "mix_post_g", "ff2_pre_g", "ff2_post_g")], axis=1)
    lamv = np.concatenate([np.broadcast_to(np.asarray(inp[k][0], f32)[None, :], (128, 128)) for k in ("lambda_q1", "lambda_k1", "lambda_q2", "lambda_k2")], axis=1)
    shared = {
        "gains": np.ascontiguousarray(gains),
        "bfg": np.ascontiguousarray(np.asarray(inp["b_forget"][0], f32).reshape(FH, 1)),
        "lamv": np.ascontiguousarray(lamv),
        "subg": pc(inp["diff_subln_g"][0]),
        "tri": np.triu(np.ones((128, 128), f32)),
        "sel": np.ascontiguousarray(np.repeat(np.eye(FH, dtype=f32), 128, axis=1)),
        "ident": np.eye(FH, dtype=f32),
        "slopes": np.zeros((128, DH), f32),
    }
    for k in ("ff1_w_gate", "ff1_w_up", "ff1_w_down", "w_in", "w_o_fox", "w_o_diff", "w_out", "ff2_w_gate", "ff2_w_up", "ff2_w_down"):
        shared[k] = np.ascontiguousarray(np.asarray(inp[k][0], f32))
    maps = []
    metaT = np.ascontiguousarray(meta.T)
    for core in range(2 * B):
        b, p = core // 2, core % 2
        own = x[b, p * TQ:(p + 1) * TQ]
        oth = x[b, (1 - p) * TQ:(2 - p) * TQ]
        m = dict(shared)
        m["xT_own"] = np.ascontiguousarray(own.T)
        m["xT_oth"] = np.ascontiguousarray(np.concatenate([oth.T, metaT], axis=1))
        kpos = np.full((128, NKB), -1.0e9, f32)
        exclk = np.full((128, NKB), NEG, f32)
        i = np.arange(128, dtype=f32)
        for kb in range(NQB):
            if p == 1:
                kpos[:, kb] = NM + kb * 128 + i
                exclk[:, kb] = 0.0
            kpos[:, NQB + 1 + kb] = NM + p * TQ + kb * 128 + i
            exclk[:, NQB + 1 + kb] = 0.0
        kpos[:NM, NQB] = np.arange(NM)
        exclk[:NM, NQB] = 0.0
        m["kpos"] = kpos
        m["exclk"] = exclk
        m["qpos0"] = np.ascontiguousarray(np.broadcast_to((NM + p * TQ + 128 * np.arange(NQB, dtype=f32))[None, :], (128, NQB)))
        m["apf"] = np.full((FH, 1), float(p), f32)
        maps.append(m)
    return maps


_CACHE = {}


def run(cfg, inp):
    key = tuple(sorted(cfg.items()))
    if key not in _CACHE:
        _CACHE[key] = Builder(cfg).build()
    nc = _CACHE[key]
    maps = host_inputs(cfg, inp)
    ncores = 2 * cfg["B"]
    res = run_bass_kernel_spmd(nc, maps, core_ids=list(range(ncores)))
    TQ, D, B = cfg["TQ"], cfg["D"], cfg["B"]
    out = np.empty((B, 2 * TQ, D), np.float32)
    for core in range(ncores):
        b, p = core // 2, core % 2
        out[b, p * TQ:(p + 1) * TQ, :] = np.asarray(res.results[core]["outT"]).T
    return out


def kernel(**inputs):
    return run(FULL, inputs)
```
